# Optimizing a Trainium2 kernel written in Bass

```python
import math
import jax, jax.numpy as jnp
from jax import lax
import numpy as np

D_MODEL = 1024
BATCH = 8
SEQ = 2048
DEPTH = 1
DEC_BATCH = 128
DEC_SEQ = 8
PAST_LEN = 16384
PAGE_SIZE = 128

HEAD_DIM = 64
N_RET_HEADS = 8
RET_WIDTH = N_RET_HEADS * HEAD_DIM
N_SWA_HEADS = 8
N_SWA_KV = 2
SWA_GROUP = N_SWA_HEADS // N_SWA_KV
SWA_WIDTH = N_SWA_HEADS * HEAD_DIM
SWA_KV_WIDTH = N_SWA_KV * HEAD_DIM
MIX_WIDTH = RET_WIDTH + SWA_WIDTH
IN_WIDTH = 4 * RET_WIDTH + SWA_WIDTH + 2 * SWA_KV_WIDTH
D_FF = 4 * D_MODEL
WINDOW = 128
RET_CHUNK = 128
EPS = 1e-6
NEG_INF = -1e30

kernel_name = "hymba_retention_swa_sink_decode_step"


def rms_norm(x, gain=None):
    xf = x.astype(jnp.float32)
    y = xf * lax.rsqrt(jnp.mean(xf * xf, axis=-1, keepdims=True) + EPS)
    if gain is not None:
        y = y * gain.astype(jnp.float32)
    return y.astype(x.dtype)


def retention_log_decay():
    return jnp.log(1.0 - 2.0 ** (-5.0 - jnp.arange(N_RET_HEADS, dtype=jnp.float32)))


def alibi_slopes():
    return 2.0 ** (-8.0 * jnp.arange(1, N_SWA_HEADS + 1, dtype=jnp.float32) / N_SWA_HEADS)


def retention(q, k, v, s0, chunk):
    b, t, h, d = q.shape
    n = t // chunk
    dt = q.dtype
    log_g = retention_log_decay()
    idx = jnp.arange(chunk, dtype=jnp.float32)
    diff = idx[:, None] - idx[None, :]
    intra = jnp.where(diff >= 0, jnp.exp(log_g[:, None, None] * jnp.maximum(diff, 0.0)), 0.0).astype(dt)
    q_decay = jnp.exp(log_g[:, None] * (idx + 1.0)).astype(dt)
    k_decay = jnp.exp(log_g[:, None] * (chunk - 1.0 - idx)).astype(dt)
    s_decay = jnp.exp(log_g * chunk).astype(dt)
    k = k * (d ** -0.5)

    def to_chunks(a):
        return a.reshape(b, n, chunk, h, a.shape[-1]).transpose(1, 0, 3, 2, 4)

    qc, kc, vc = to_chunks(q), to_chunks(k), to_chunks(v)

    def step(s, inp):
        qi, ki, vi = inp
        scores = jnp.einsum('bhld,bhmd->bhlm', qi, ki) * intra
        o = (jnp.einsum('bhlm,bhme->bhle', scores, vi)
             + jnp.einsum('bhld,bhde->bhle', qi * q_decay[..., None], s))
        s = s * s_decay[:, None, None] + jnp.einsum('bhld,bhle->bhde', ki * k_decay[..., None], vi)
        return s, o

    s, o = lax.scan(step, s0.astype(dt), (qc, kc, vc))
    o = o.transpose(1, 0, 3, 2, 4).reshape(b, t, h, v.shape[-1])
    return o, s


def sink_softmax(scores, sinks):
    sink = sinks.astype(jnp.float32).reshape(N_SWA_KV, SWA_GROUP)[:, :, None, None]
    sink_col = jnp.broadcast_to(sink, scores.shape[:-1] + (1,))
    p = jax.nn.softmax(jnp.concatenate([scores, sink_col], axis=-1), axis=-1)
    return p[..., :-1]


def swa_prompt(q, k, v, sinks):
    b, t, h, d = q.shape
    w = WINDOW
    n = t // w
    qb = q.reshape(b, n, w, N_SWA_KV, SWA_GROUP, d)
    kb = k.reshape(b, n, w, N_SWA_KV, d)
    vb = v.reshape(b, n, w, N_SWA_KV, d)
    pad = jnp.zeros_like(kb[:, :1])
    kk = jnp.concatenate([jnp.concatenate([pad, kb[:, :-1]], axis=1), kb], axis=2)
    vv = jnp.concatenate([jnp.concatenate([pad.astype(vb.dtype), vb[:, :-1]], axis=1), vb], axis=2)
    i = jnp.arange(w)[:, None]
    j = jnp.arange(2 * w)[None, :]
    dist = w + i - j
    key_pos = jnp.arange(n)[:, None, None] * w - w + j[None]
    valid = (dist >= 0)[None] & (dist < w)[None] & (key_pos >= 0)
    slopes = alibi_slopes().reshape(N_SWA_KV, SWA_GROUP)[:, :, None, None]
    scores = jnp.einsum('bnqkgd,bnskd->bnkgqs', qb, kk).astype(jnp.float32) * (d ** -0.5)
    scores = scores - slopes * dist.astype(jnp.float32)
    scores = jnp.where(valid[None, :, None, None], scores, NEG_INF)
    probs = sink_softmax(scores, sinks).astype(v.dtype)
    out = jnp.einsum('bnkgqs,bnskd->bnqkgd', probs, vv)
    return out.reshape(b, t, h * d)


def swa_sample(q, k, v, kbuf, vbuf, sinks):
    b, t, h, d = q.shape
    wb = kbuf.shape[1]
    kk = jnp.concatenate([kbuf.astype(k.dtype), k], axis=1)
    vv = jnp.concatenate([vbuf.astype(v.dtype), v], axis=1)
    i = jnp.arange(t)[:, None]
    j = jnp.arange(wb + t)[None, :]
    dist = wb + i - j
    valid = (dist >= 0) & (dist < WINDOW)
    slopes = alibi_slopes().reshape(N_SWA_KV, SWA_GROUP)[:, :, None, None]
    qg = q.reshape(b, t, N_SWA_KV, SWA_GROUP, d)
    scores = jnp.einsum('bqkgd,bskd->bkgqs', qg, kk).astype(jnp.float32) * (d ** -0.5)
    scores = scores - slopes * dist.astype(jnp.float32)
    scores = jnp.where(valid, scores, NEG_INF)
    probs = sink_softmax(scores, sinks).astype(v.dtype)
    out = jnp.einsum('bkgqs,bskd->bqkgd', probs, vv).reshape(b, t, h * d)
    return out, kk[:, -wb:], vv[:, -wb:]


def decoder_layer(x, ret_state, win_k, win_v, norm_mix_gain, w_in, q_norm_gain, k_norm_gain,
                  attn_sinks, w_out, norm_ffn_gain, w_up, w_down):
    b, t, _ = x.shape
    hn = rms_norm(x, norm_mix_gain)
    proj = hn @ w_in
    cuts = np.cumsum([RET_WIDTH] * 4 + [SWA_WIDTH, SWA_KV_WIDTH]).tolist()
    q_r, k_r, v_r, g_r, q_s, k_s, v_s = jnp.split(proj, cuts, axis=-1)

    q_r = q_r.reshape(b, t, N_RET_HEADS, HEAD_DIM)
    k_r = k_r.reshape(b, t, N_RET_HEADS, HEAD_DIM)
    v_r = v_r.reshape(b, t, N_RET_HEADS, HEAD_DIM)
    if ret_state is None:
        ret_state = jnp.zeros((b, N_RET_HEADS, HEAD_DIM, HEAD_DIM), x.dtype)
    chunk = RET_CHUNK if t % RET_CHUNK == 0 else t
    o_r, new_ret = retention(q_r, k_r, v_r, ret_state, chunk)
    o_r = rms_norm(o_r).reshape(b, t, RET_WIDTH) * jax.nn.silu(g_r)

    q_s = rms_norm(q_s.reshape(b, t, N_SWA_HEADS, HEAD_DIM), q_norm_gain)
    k_s = rms_norm(k_s.reshape(b, t, N_SWA_KV, HEAD_DIM), k_norm_gain)
    v_s = v_s.reshape(b, t, N_SWA_KV, HEAD_DIM)
    if win_k is None:
        o_s = swa_prompt(q_s, k_s, v_s, attn_sinks)
        new_k, new_v = k_s[:, -WINDOW:], v_s[:, -WINDOW:]
    else:
        o_s, new_k, new_v = swa_sample(q_s, k_s, v_s, win_k, win_v, attn_sinks)

    h = x + jnp.concatenate([o_r, o_s], axis=-1) @ w_out
    hf = rms_norm(h, norm_ffn_gain)
    h = h + jnp.square(jax.nn.relu(hf @ w_up)) @ w_down
    return h, new_ret, new_k, new_v


def setup_inputs(seed: int = 0) -> dict:
    key = jax.random.key(seed)
    ks = jax.random.split(key, 14)
    wb = min(WINDOW, PAST_LEN)
    f32 = jnp.float32
    return {
        "x_prompt": jax.random.normal(ks[0], (BATCH, SEQ, D_MODEL), f32),
        "x_sample": jax.random.normal(ks[1], (DEC_BATCH, DEC_SEQ, D_MODEL), f32),
        "state_ret": 0.5 * jax.random.normal(ks[2], (DEC_BATCH, N_RET_HEADS, HEAD_DIM, HEAD_DIM), f32),
        "cache_swa_k": jax.random.normal(ks[3], (DEC_BATCH, wb, N_SWA_KV, HEAD_DIM), f32),
        "cache_swa_v": jax.random.normal(ks[4], (DEC_BATCH, wb, N_SWA_KV, HEAD_DIM), f32),
        "norm_mix_gain": 1.0 + 0.02 * jax.random.normal(ks[5], (D_MODEL,), f32),
        "w_in": jax.random.normal(ks[6], (D_MODEL, IN_WIDTH), f32) * D_MODEL ** -0.5,
        "q_norm_gain": 1.0 + 0.02 * jax.random.normal(ks[7], (HEAD_DIM,), f32),
        "k_norm_gain": 1.0 + 0.02 * jax.random.normal(ks[8], (HEAD_DIM,), f32),
        "attn_sinks": jax.random.normal(ks[9], (N_SWA_HEADS,), f32),
        "w_out": jax.random.normal(ks[10], (MIX_WIDTH, D_MODEL), f32) * MIX_WIDTH ** -0.5,
        "norm_ffn_gain": 1.0 + 0.02 * jax.random.normal(ks[11], (D_MODEL,), f32),
        "w_up": jax.random.normal(ks[12], (D_MODEL, D_FF), f32) * D_MODEL ** -0.5,
        "w_down": jax.random.normal(ks[13], (D_FF, D_MODEL), f32) * D_FF ** -0.5,
    }


def reference(x_prompt, x_sample, state_ret, cache_swa_k, cache_swa_v, norm_mix_gain, w_in,
              q_norm_gain, k_norm_gain, attn_sinks, w_out, norm_ffn_gain, w_up, w_down):
    y_prompt, y_sample = x_prompt, x_sample
    for _ in range(DEPTH):
        y_prompt, ret_p, k_p, v_p = decoder_layer(
            y_prompt, None, None, None, norm_mix_gain, w_in, q_norm_gain, k_norm_gain,
            attn_sinks, w_out, norm_ffn_gain, w_up, w_down)
        y_sample, ret_s, k_s, v_s = decoder_layer(
            y_sample, state_ret, cache_swa_k, cache_swa_v, norm_mix_gain, w_in, q_norm_gain,
            k_norm_gain, attn_sinks, w_out, norm_ffn_gain, w_up, w_down)
    return (y_prompt, y_sample, ret_p, k_p, v_p, ret_s, k_s, v_s)
```

```python
import numpy as np
import concourse.bass as bass
import concourse.mybir as mybir

F32 = mybir.dt.float32
BF16 = mybir.dt.bfloat16
AF = mybir.ActivationFunctionType
ALU = mybir.AluOpType
AX = mybir.AxisListType

SAME_ENGINE_WAR_SEM = False
PE, ACT, DVE, POOL, SP = "pe", "act", "dve", "pool", "sp"
COMPUTE = (PE, ACT, DVE, POOL)
ENGS = (PE, ACT, DVE, POOL, SP)


class Tile:
    def __init__(self, prog, handle, name):
        self.prog = prog
        self.h = handle
        self.name = name
        self.last_w = []
        self.readers = []
        self.last_acc = {}
        self.const = False
        self.excl = False
        self.sem = None
        self.ndma = 0

    def __getitem__(self, key):
        return View(self, self.h.ap()[key])

    @property
    def full(self):
        return View(self, self.h.ap())


class View:
    def __init__(self, tile, ap):
        self.tile = tile
        self.ap = ap

    def __getitem__(self, key):
        return View(self.tile, self.ap[key])

    def bitcast(self, dt):
        return View(self.tile, self.ap.bitcast(dt))

    def rearrange(self, s, **kw):
        return View(self.tile, self.ap.rearrange(s, **kw))


class Op:
    __slots__ = ("eng", "lidx", "fn", "preds", "raw", "signal", "is_dma", "sem", "target", "tag",
                 "dur", "lat", "pos", "fin", "nsucc", "sync", "cyc")

    def __init__(self, eng, lidx, fn, tag=""):
        self.eng = eng
        self.lidx = lidx
        self.fn = fn
        self.preds = set()
        self.raw = set()
        self.signal = False
        self.is_dma = False
        self.sem = None
        self.target = 0
        self.tag = tag
        self.dur = 0.1
        self.lat = 0.0
        self.pos = -1
        self.fin = 0.0
        self.cyc = 0.0


class Prog:
    def __init__(self, nc, schedule=True, window=200):
        self.nc = nc
        self.all = []
        self.tiles = []
        self.schedule = schedule
        self.window = window
        self.pin = False
        self.last_pin = {}
        self.sync_lat = 1.2
        self.pe_scale = 1.0
        self.pace = False
        self.pace_hi = 0.55
        self.pace_lo = 0.42

    def tile(self, handle, name):
        t = Tile(self, handle, name)
        self.tiles.append(t)
        return t

    def add(self, eng, fn, reads=(), writes=(), dma=False, tag="", dur=0.1, lat=0.0, cyc=0.0):
        op = Op(eng, len(self.all), fn, tag)
        op.cyc = cyc
        op.is_dma = dma
        op.dur = dur * (self.pe_scale if eng == PE else 1.0)
        op.lat = lat
        op.sync = self.sync_lat
        rt, wt = [], []
        for x in reads:
            t = x.tile if isinstance(x, View) else x
            if t is not None and t not in rt:
                rt.append(t)
        for x in writes:
            t = x.tile if isinstance(x, View) else x
            if t is not None and t not in wt:
                wt.append(t)
        for t in rt:
            if t in wt:
                continue
            for w in t.last_w:
                op.preds.add(w)
                op.raw.add(w)
            if t.excl:
                for r in t.readers:
                    if r.eng != eng:
                        op.preds.add(r)
        for t in wt:
            for w in t.last_w:
                op.preds.add(w)
                if t in rt or SAME_ENGINE_WAR_SEM:
                    op.raw.add(w)
            for r in t.readers:
                op.preds.add(r)
                if SAME_ENGINE_WAR_SEM:
                    op.raw.add(r)
        for t in rt + wt:
            if not t.const:
                prev = t.last_acc.get(eng)
                if prev is not None:
                    op.preds.add(prev)
                t.last_acc[eng] = op
        if self.pin:
            prev = self.last_pin.get(eng)
            if prev is not None:
                op.preds.add(prev)
            self.last_pin[eng] = op
        op.preds.discard(op)
        if dma:
            owner = None
            for t in wt + rt:
                if t.h is not None:
                    owner = t
                    break
            if owner is None:
                owner = (wt + rt)[0]
            op.sem = owner
            owner.ndma += 1
            op.target = 16 * owner.ndma
        for t in rt:
            if t not in wt:
                t.readers.append(op)
        for t in wt:
            t.last_w = [op]
            t.readers = []
        self.all.append(op)
        return op

    @staticmethod
    def _needs_sem(op, q):
        if q.is_dma:
            return True
        if q.eng != op.eng or op.is_dma:
            return True
        return (q in op.raw) and op.eng != PE

    def _schedule(self):
        order = {e: [] for e in ENGS}
        if not self.schedule:
            for op in self.all:
                op.pos = len(order[op.eng])
                order[op.eng].append(op)
            return order
        succs = {}
        npend = {}
        ready = {}
        for op in self.all:
            npend[op] = len(op.preds)
            ready[op] = 0.0
            for q in sorted(op.preds, key=lambda o: o.lidx):
                succs.setdefault(q, []).append(op)
        tail = {}
        for op in reversed(self.all):
            t_ = 0.0
            for s_ in succs.get(op, ()):
                v = tail[s_] + (s_.sync if self._needs_sem(s_, op) else 0.0)
                if v > t_:
                    t_ = v
            tail[op] = t_ + op.dur + op.lat
        pend = {e: [op for op in self.all if op.eng == e] for e in ENGS}
        head = {e: 0 for e in ENGS}
        t_free = {e: 0.0 for e in ENGS}
        nleft = len(self.all)
        dma_free = 0.0
        hist = []
        PACE_W = 3.4
        while nleft:
            best, best_t, best_i = None, None, None
            for e in ENGS:
                lst = pend[e]
                h = head[e]
                while h < len(lst) and lst[h] is None:
                    h += 1
                head[e] = h
                cnt = 0
                i = h
                tf = t_free[e]
                pe_cand = None
                while i < len(lst) and cnt < self.window:
                    op = lst[i]
                    i += 1
                    if op is None:
                        continue
                    cnt += 1
                    if npend[op] > 0:
                        continue
                    st = ready[op] if ready[op] > tf else tf
                    if e == PE and self.pace and st <= tf + 1e-9:
                        if pe_cand is None:
                            pe_cand = []
                        pe_cand.append((op, st, i - 1))
                        continue
                    if best is None or st < best_t - 1e-9 or (abs(st - best_t) <= 1e-9 and tail[op] > tail[best]):
                        best, best_t, best_i = op, st, i - 1
                if e == PE and pe_cand:
                    while hist and hist[0][0] < tf - PACE_W:
                        hist.pop(0)
                    ratio = sum(c for _, c in hist) / (PACE_W * 2400.0)
                    if ratio > self.pace_hi:
                        pick = min(pe_cand, key=lambda x: (x[0].cyc / max(x[0].dur, 1e-3), -tail[x[0]]))
                    elif ratio < self.pace_lo:
                        pick = max(pe_cand, key=lambda x: (x[0].cyc / max(x[0].dur, 1e-3), tail[x[0]]))
                    else:
                        pick = max(pe_cand, key=lambda x: tail[x[0]])
                    op, st, ii = pick
                    if best is None or st < best_t - 1e-9 or (abs(st - best_t) <= 1e-9 and tail[op] > tail[best]):
                        best, best_t, best_i = op, st, ii
            assert best is not None, "scheduler deadlock"
            e = best.eng
            best.pos = len(order[e])
            order[e].append(best)
            t_free[e] = best_t + best.dur
            best.fin = best_t + best.dur + best.lat
            if e == PE:
                hist.append((t_free[e], best.cyc))
            pend[e][best_i] = None
            nleft -= 1
            for s_ in succs.get(best, ()):
                npend[s_] -= 1
                r = best.fin + (s_.sync if self._needs_sem(s_, best) else 0.0)
                if r > ready[s_]:
                    ready[s_] = r
        self.est = max(t_free.values())
        return order

    def emit(self):
        nc = self.nc
        order = self._schedule()
        for op in self.all:
            for q in op.preds:
                if q.is_dma:
                    continue
                if self._needs_sem(op, q):
                    q.signal = True
                else:
                    assert q.pos < op.pos, ("same-engine order violated", q.tag, op.tag)
        cnt = {}
        for e in ENGS:
            c = 0
            arr = []
            for op in order[e]:
                if op.signal and not op.is_dma:
                    c += 1
                arr.append(c)
            cnt[e] = arr
        sems = {e: nc.alloc_semaphore(name=f"s_{e}") for e in ENGS}
        for t in self.tiles:
            if t.ndma > 0:
                t.sem = nc.alloc_semaphore(name=f"d_{t.name}")
        stats = {e: [0, 0] for e in ENGS}
        with nc.Block() as block:
            def run(e):
                def body(h):
                    seen = {}
                    seen_d = {}
                    for op in order[e]:
                        need = {}
                        for q in sorted(op.preds, key=lambda o: o.lidx):
                            if q.is_dma:
                                k = id(q.sem)
                                if seen_d.get(k, 0) < q.target:
                                    h.wait_ge(q.sem.sem, q.target)
                                    seen_d[k] = q.target
                                    stats[e][1] += 1
                            elif self._needs_sem(op, q):
                                v = cnt[q.eng][q.pos]
                                if need.get(q.eng, 0) < v:
                                    need[q.eng] = v
                        for de, v in need.items():
                            if seen.get(de, 0) < v:
                                h.wait_ge(sems[de], v)
                                seen[de] = v
                                stats[e][1] += 1
                        ins = op.fn(h)
                        stats[e][0] += 1
                        if op.is_dma:
                            ins.then_inc(op.sem.sem, 16)
                        elif op.signal:
                            ins.then_inc(sems[e], 1)
                    for op in order[e]:
                        if op.is_dma:
                            k = id(op.sem)
                            if seen_d.get(k, 0) < op.target:
                                h.wait_ge(op.sem.sem, op.target)
                                seen_d[k] = op.target
                return body

            block.tensor(run(PE))
            block.scalar(run(ACT))
            block.vector(run(DVE))
            block.gpsimd(run(POOL))
            block.sync(run(SP))
        self.stats = stats
        return stats


def _ap(x):
    return x.ap if isinstance(x, View) else x


def _views(*xs):
    return [x for x in xs if isinstance(x, View)]


def _fd(x):
    a = _ap(x)
    n = 1
    for d in a.shape[1:]:
        n *= d
    return n


def _cost(eng, fd, f32=True):
    if eng == ACT:
        return 0.22 + fd * 0.00083
    if eng == DVE:
        return 0.08 + fd * (0.00115 if f32 else 0.0007)
    if eng == POOL:
        return 0.25 + fd * 0.0035
    return 0.1


def mm(p, out, lhsT, rhs, start=True, stop=True, **kw):
    n = _fd(rhs)
    f32 = _ap(rhs).dtype == F32
    d = (max(n, 64) / 2400.0 + 0.03) * (4.0 if f32 else 1.0)
    return p.add(PE, lambda h: h.matmul(_ap(out), _ap(lhsT), _ap(rhs), start=start, stop=stop, **kw),
                 reads=_views(lhsT, rhs), writes=_views(out), tag="mm", dur=d, lat=0.25,
                 cyc=n * (4.0 if f32 else 1.0))


def tr(p, out, in_, ident):
    f32 = _ap(in_).dtype == F32
    d = 0.12 * (4.0 if f32 else 1.0)
    return p.add(PE, lambda h: h.transpose(_ap(out), _ap(in_), _ap(ident)),
                 reads=_views(in_, ident), writes=_views(out), tag="tr", dur=d, lat=0.25, cyc=128.0)


def act(p, out, in_, func, bias=None, scale=None, accum_out=None, eng=ACT):
    kw = {}
    if bias is not None:
        kw["bias"] = _ap(bias)
    if scale is not None:
        kw["scale"] = _ap(scale)
    if accum_out is not None:
        kw["accum_out"] = _ap(accum_out)
    return p.add(eng, lambda h: h.activation(_ap(out), _ap(in_), func, **kw),
                 reads=_views(in_, bias, scale), writes=_views(out, accum_out), tag="act", dur=_cost(ACT, _fd(out)))


def tt(p, eng, out, in0, in1, op):
    return p.add(eng, lambda h: h.tensor_tensor(_ap(out), _ap(in0), _ap(in1), op),
                 reads=_views(in0, in1), writes=_views(out), tag="tt", dur=_cost(eng, _fd(out)))


def ts(p, eng, out, in0, s1, s2, op0, op1=None, accum_out=None):
    kw = {}
    if op1 is not None:
        kw["op1"] = op1
    if accum_out is not None:
        kw["accum_out"] = _ap(accum_out)
    return p.add(eng, lambda h: h.tensor_scalar(_ap(out), _ap(in0), _ap(s1), _ap(s2) if s2 is not None else None, op0, **kw),
                 reads=_views(in0, s1, s2), writes=_views(out, accum_out), tag="ts", dur=_cost(eng, _fd(out), False))


def stt(p, eng, out, in0, scalar, in1, op0, op1):
    return p.add(eng, lambda h: h.scalar_tensor_tensor(_ap(out), _ap(in0), _ap(scalar), _ap(in1), op0, op1),
                 reads=_views(in0, scalar, in1), writes=_views(out), tag="stt", dur=_cost(eng, _fd(out)))


def cp(p, eng, out, in_):
    d = _cost(eng, _fd(out), _ap(in_).dtype == F32)
    if eng == ACT:
        return p.add(eng, lambda h: h.copy(_ap(out), _ap(in_)), reads=_views(in_), writes=_views(out), tag="cp", dur=d)
    return p.add(eng, lambda h: h.tensor_copy(_ap(out), _ap(in_)), reads=_views(in_), writes=_views(out), tag="cp", dur=d)


def red(p, eng, out, in_, op=None, axis=None):
    op = op or ALU.add
    axis = axis or AX.X
    return p.add(eng, lambda h: h.tensor_reduce(_ap(out), _ap(in_), axis, op),
                 reads=_views(in_), writes=_views(out), tag="red", dur=_cost(eng, _fd(in_)))


def mset(p, eng, out, val):
    return p.add(eng, lambda h: h.memset(_ap(out), val), reads=(), writes=_views(out), tag="mset", dur=_cost(eng, _fd(out), False) * 0.5)


def recip(p, out, in_):
    return p.add(DVE, lambda h: h.reciprocal(_ap(out), _ap(in_)), reads=_views(in_), writes=_views(out), tag="rcp",
                 dur=0.1 + _fd(out) * 0.0065)


def dma(p, eng, out, in_, **kw):
    a = _ap(out)
    nbytes = 1
    for d in a.shape:
        nbytes *= d
    nbytes *= 4
    issue = 1.1 if eng == POOL else 0.12
    return p.add(eng, lambda h: h.dma_start(out=_ap(out), in_=_ap(in_), **kw),
                 reads=_views(in_), writes=_views(out), dma=True, tag="dma", dur=issue, lat=2.0 + nbytes / 150e3)


from concourse.bass_utils import run_bass_kernel_spmd

NCORES = 8
PACE = False
SYNC_SA = 1.2
SYNC_B = 1.2
PIN_B = False
D = 1024
NT_P = 16
NTILES = 17
INW = 2816
DFF = 4096
CG = [(0, 512), (512, 1024), (1024, 1536), (1536, 2048), (2048, 2560), (2560, 2816)]


def _dsize(dt):
    return 4 if dt == F32 else 2


class Arena:
    def __init__(self, nc, prog, base, top):
        self.nc, self.p, self.base, self.top, self.cur = nc, prog, base, top, base
        self.live = []
        self.dead = []
        self.peak = base

    def alloc(self, name, shape, dt):
        nb = int(np.prod(shape[1:])) * _dsize(dt)
        off = (self.cur + 31) // 32 * 32
        assert off + nb <= self.top, f"SBUF arena overflow at {name}: {off + nb - self.top} B over"
        h = self.nc.alloc_sbuf_tensor_at(name, list(shape), dt, offset=off)
        self.cur = off + nb
        self.peak = max(self.peak, self.cur)
        t = self.p.tile(h, name)
        t.off, t.nb = off, nb
        seen_ = set()
        for (o_, n_, ops_) in self.dead:
            if o_ < off + nb and off < o_ + n_:
                for q in ops_:
                    if id(q) not in seen_:
                        seen_.add(id(q))
                        t.readers.append(q)
        self.live.append(t)
        return t

    def mark(self):
        return (self.cur, len(self.live))

    def reset(self, mark):
        cur, n = mark
        for t in self.live[n:]:
            self.dead.append((t.off, t.nb, list(t.last_w) + list(t.readers)))
        del self.live[n:]
        self.cur = cur


def bc(view, shape, axis):
    return View(view.tile, view.ap.unsqueeze(axis).to_broadcast(list(shape)))


def build_program(debug=False, stop=None, nA=NT_P, skipS=False, schedule=True, window=200):
    nc = bass.Bass("TRN2", target_bir_lowering=False)
    p = Prog(nc, schedule=schedule, window=window)
    p.sync_lat = SYNC_SA
    p.pe_scale = 1.3
    p.pace = PACE

    def din(name, shape, dt=F32):
        return nc.dram_tensor(name, list(shape), dt, kind="ExternalInput").ap()

    def dout(name, shape):
        return nc.dram_tensor(name, list(shape), F32, kind="ExternalOutput").ap()

    xp = din("xp", [2048, D]); xs = din("xs", [128, D])
    s0 = din("s0", [16, 8, 64, 64]); ck = din("ck", [16, 128, 128]); cv = din("cv", [16, 128, 128])
    w_in = din("w_in", [D, INW]); w_out = din("w_out", [D, D]); w_up = din("w_up", [D, DFF]); w_down = din("w_down", [DFF, D])
    v_gmix = din("v_gmix", [128, D]); v_gffn = din("v_gffn", [128, D])
    v_qk = din("v_qk", [128, 128]); v_sink = din("v_sink", [128, 8])
    c_id = din("c_id", [128, 128]); c_mask = din("c_mask", [128, 256]); c_sc = din("c_sc", [128, 48])
    c_g = din("c_g", [64, 16]); c_bias = din("c_bias", [128, 3 * 8 * 128]); c_bm = din("c_bm", [128, 16])
    yp = dout("yp", [2048, D]); ys = dout("ys", [128, D])
    rp = dout("rp", [8, 64, 64]); kp = dout("kp", [128, 128]); vp = dout("vp", [128, 128])
    rs = dout("rs", [16, 8, 64, 64]); kso = dout("kso", [16, 128, 128]); vso = dout("vso", [16, 128, 128])
    if debug:
        dbg_mix = dout("dbg_mix", [128, D]); dbg_ot = dout("dbg_ot", [128, 512]); dbg_gq = dout("dbg_gq", [128, 512])

    base = (nc.sbuf_base + 31) // 32 * 32
    top = nc.sbuf_top
    pers = Arena(nc, p, base, top)
    ident = pers.alloc("ident", [128, 128], BF16)
    ident32 = pers.alloc("ident32", [128, 128], F32)
    gffn = pers.alloc("gffn", [128, D], F32)
    mixTp = [pers.alloc(f"mixTp{i}", [128, 2 if i < 8 else 1, 8, 128], BF16) for i in range(9)]

    def mv(t):
        return mixTp[t // 2][:, t % 2]
    ar = Arena(nc, p, pers.cur, top)

    banks = [p.tile(nc.alloc_psum_tensor(f"bank{i}", [128, 512], F32), f"bank{i}") for i in range(8)]
    for b_ in banks:
        b_.excl = True

    class Rot:
        def __init__(self, idx):
            self.idx, self.i = idx, 0

        def next(self):
            b = banks[self.idx[self.i % len(self.idx)]]
            self.i += 1
            return b

    w_in_t = [ar.alloc(f"w_in{g}", [128, 8, c1 - c0], BF16) for g, (c0, c1) in enumerate(CG)]
    gmix = ar.alloc("gmix", [128, D], F32)
    cmask = ar.alloc("cmask", [128, 256], BF16)
    csc = ar.alloc("csc", [128, 48], F32)
    cg_t = ar.alloc("cg", [64, 16], F32)
    cbias = ar.alloc("cbias", [128, 3 * 8 * 128], F32)
    vqk = ar.alloc("vqk", [128, 128], F32)
    esink = ar.alloc("esink", [128, 8], F32)
    xt = [ar.alloc(f"xt{i}", [128, D], F32) for i in range(2)]
    junk32 = ar.alloc("junk32", [128, 512], F32)
    sm = ar.alloc("sm", [128, 4], F32)
    hn = ar.alloc("hn", [128, D], BF16)
    hnT = [ar.alloc(f"hnT{i}", [128, 8, 128], BF16) for i in range(2)]
    qr_ = [ar.alloc(f"qr{i}", [128, 512], BF16) for i in range(2)]
    kr_ = [ar.alloc(f"kr{i}", [128, 512], BF16) for i in range(2)]
    vr_ = [ar.alloc(f"vr{i}", [128, 512], BF16) for i in range(2)]
    eg = ar.alloc("eg", [128, 512], F32)
    gq_ = [ar.alloc(f"gq{i}", [128, 512], F32) for i in range(2)]
    ssq = ar.alloc("ssq", [128, 10], F32)
    rq = ar.alloc("rq", [128, 10], F32)
    qs_t = junk32
    qs = ar.alloc("qs", [128, 512], BF16)
    ks_t = ar.alloc("ks_t", [128, 128], F32)
    ks32 = ar.alloc("ks32", [128, 128], F32)
    ksb = ar.alloc("ksb", [128, 128], BF16)
    vs32 = ar.alloc("vs32", [128, 128], F32)
    vaug = [ar.alloc(f"vaug{i}", [128, 2, 66], BF16) for i in range(3)]
    qT_r_ = [ar.alloc(f"qT_r{i}", [64, 8, 128], BF16) for i in range(2)]
    kT_r_ = [ar.alloc(f"kT_r{i}", [64, 8, 128], BF16) for i in range(2)]
    qT_s_ = [ar.alloc(f"qT_s{i}", [64, 8, 128], BF16) for i in range(2)]
    kT_s = [ar.alloc(f"kT_s{i}", [64, 2, 128], BF16) for i in range(3)]
    scm = ar.alloc("scm", [128, 8, 128], BF16)
    sso = ar.alloc("sso", [128, 8], F32)
    fo = ar.alloc("fo", [128, 8], F32)
    ot = ar.alloc("ot", [128, 512], F32)
    mix = ar.alloc("mix", [128, D], BF16)
    tb = [ar.alloc(f"tb{i}", [128, 512], F32) for i in range(2)]
    pT = [[ar.alloc(f"pT{j}{b}", [128, 512], BF16) for b in range(2)] for j in range(2)]
    den = ar.alloc("den", [128, 8], F32)
    rden = ar.alloc("rden", [128, 8], F32)
    mark_sa = ar.mark()

    poolA = Rot([0, 1, 2])
    poolF = Rot([3, 4])
    poolB = poolF
    poolC = Rot([5, 6, 7])

    dma(p, SP, xt[0].full, xs)
    dma(p, POOL, ident.full, c_id)
    dma(p, SP, ident32.full, c_id)
    dma(p, POOL, cmask.full, c_mask)
    dma(p, SP, csc.full, c_sc)
    dma(p, SP, cg_t.full, c_g)
    dma(p, SP, gmix.full, v_gmix)
    dma(p, SP, vqk.full, v_qk)
    dma(p, SP, esink.full, v_sink)
    w_in_v = w_in.rearrange("(k p) n -> p k n", p=128)
    for g in (4, 5, 0, 1, 2, 3):
        c0, c1 = CG[g]
        dma(p, POOL, w_in_t[g].full, w_in_v[:, :, c0:c1])
    dma(p, SP, cbias.full, c_bias)
    dma(p, SP, gffn.full, v_gffn)
    w_out_v = w_out.rearrange("(k p) n -> p k n", p=128)
    for i in range(3):
        mset(p, POOL, vaug[i].full, 1.0)

    for t_ in [ident, ident32, gffn, gmix, cmask, csc, cg_t, cbias, vqk] + w_in_t:
        t_.const = True
    kscale = [csc[:, 0:8], csc[:, 24:32]]
    oscale = [csc[:, 8:16], csc[:, 32:40]]
    osc2 = [csc[:, 16:24], csc[:, 40:48]]
    cm = [cmask[:, 0:128], cmask[:, 128:256]]
    g128 = cg_t[:, 0:8]
    g8 = cg_t[:, 8:16]
    bias_cur = cbias[:, 0:1024]
    bias_prev = cbias[:, 1024:2048]
    bias_sn = cbias[:, 2048:3072]
    qg_bc = vqk[:, 0:64]
    kg_bc = vqk[:, 64:128]

    def r3(v, a, b):
        return v.rearrange("p (a b) -> p a b", a=a, b=b)

    def front(x_ap, s, hn32=None, load=True):
        if load:
            dma(p, SP, xt[s].full, x_ap)
        act(p, hn.full, xt[s].full, AF.Square, scale=1.0 / 32.0, accum_out=sm[:, 0:1])
        act(p, sm[:, 1:2], sm[:, 0:1], AF.Ln, bias=1e-6)
        act(p, sm[:, 2:3], sm[:, 1:2], AF.Exp, scale=-0.5)
        if hn32 is None:
            stt(p, DVE, hn.full, xt[s].full, sm[:, 2:3], gmix.full, ALU.mult, ALU.mult)
        else:
            stt(p, DVE, hn32.full, xt[s].full, sm[:, 2:3], gmix.full, ALU.mult, ALU.mult)
            cp(p, ACT, hn.full, hn32.full)
        b = poolF.next()
        bv = b.full.bitcast(BF16)
        for k in range(8):
            tr(p, bv[:, k * 128:(k + 1) * 128], hn[:, k * 128:(k + 1) * 128], ident.full)
        cp(p, DVE, hnT[s].full.rearrange("p k t -> p (k t)"), bv)

    def inproj_stages(s, v, vslot):
        qr, kr, vr, gq = qr_[s], kr_[s], vr_[s], gq_[s]
        qT_r, kT_r, qT_s = qT_r_[s], kT_r_[s], qT_s_[s]
        pj = {}

        def group(g):
            c0, c1 = CG[g]
            b = poolA.next()
            n = c1 - c0
            for k in range(8):
                mm(p, b[:, 0:n], hnT[s][:, k, :], w_in_t[g][:, k, :], start=(k == 0), stop=(k == 7))
            pj[g] = b
            if g == 0:
                cp(p, ACT, qr.full, b.full)
            elif g == 1:
                tt(p, DVE, r3(kr.full, 8, 64), r3(b.full, 8, 64), bc(kscale[v], [128, 8, 64], 2), ALU.mult)
            elif g == 2:
                cp(p, ACT, vr.full, b.full)
            elif g == 3:
                act(p, eg.full, b.full, AF.Exp, scale=-1.0)
                act(p, eg.full, eg.full, AF.Ln, bias=1.0)
                act(p, eg.full, eg.full, AF.Exp, scale=-1.0)
                tt(p, DVE, gq.full, eg.full, b.full, ALU.mult)
            elif g == 4:
                act(p, junk32.full, b.full, AF.Square, scale=0.125)
                red(p, DVE, ssq[:, 0:8], r3(junk32.full, 8, 64))
            elif g == 5:
                act(p, junk32[:, 0:128], b[:, 0:128], AF.Square, scale=0.125)
                red(p, DVE, ssq[:, 8:10], r3(junk32[:, 0:128], 2, 64))
                act(p, rq.full, ssq.full, AF.Ln, bias=1e-6)
                act(p, rq.full, rq.full, AF.Exp, scale=-0.5)
                b4 = pj[4]
                stt(p, DVE, r3(qs.full, 8, 64), r3(b4.full, 8, 64), 0.125, bc(rq[:, 0:8], [128, 8, 64], 2), ALU.mult, ALU.mult)
                tt(p, DVE, r3(ks_t.full, 2, 64), r3(b[:, 0:128], 2, 64), bc(rq[:, 8:10], [128, 2, 64], 2), ALU.mult)
                tt(p, DVE, r3(ks32.full, 2, 64), r3(ks_t.full, 2, 64), bc(kg_bc, [128, 2, 64], 1), ALU.mult)
                tt(p, POOL, r3(ksb.full, 2, 64), r3(ks32.full, 2, 64), bc(qg_bc, [128, 2, 64], 1), ALU.mult)
                cp(p, ACT, vs32.full, b[:, 128:256])
                cp(p, POOL, vaug[vslot][:, :, 0:64], r3(vs32.full, 2, 64))

        def tgroup(src, dst, nh, eng):
            b = poolB.next()
            bv = b.full.bitcast(BF16)
            for h in range(nh):
                tr(p, bv[0:64, h * 128:(h + 1) * 128], src[:, h * 64:(h + 1) * 64], ident.full)
            cp(p, eng, dst.full.rearrange("p h t -> p (h t)"), bv[0:64, 0:nh * 128])

        def P0():
            group(4); group(5); group(0)

        def P1():
            group(1); group(2); group(3)

        def T0():
            tgroup(qr, qT_r, 8, DVE)
            tgroup(kr, kT_r, 8, ACT)

        def T1():
            tgroup(qs, qT_s, 8, DVE)
            tgroup(ksb, kT_s[vslot], 2, ACT)

        return [P0, P1, T0, T1]

    def inproj(s, v, vslot):
        for f_ in inproj_stages(s, v, vslot):
            f_()

    def ret_scores(v, s=0):
        kT_r, qT_r = kT_r_[s], qT_r_[s]
        for half in range(2):
            b = poolC.next()
            for hh in range(4):
                h = half * 4 + hh
                mm(p, b[:, hh * 128:(hh + 1) * 128], kT_r[:, h, :], qT_r[:, h, :])
            tt(p, DVE, scm[:, half * 4:(half + 1) * 4, :], r3(b.full, 4, 128), bc(cm[v], [128, 4, 128], 1), ALU.mult)

    def ret_out(ob, v, s=0):
        gq = gq_[s]
        act(p, ot.full, ob.full, AF.Square)
        red(p, DVE, sso.full, r3(ot.full, 8, 64))
        tt(p, DVE, sso.full, sso.full, osc2[v], ALU.mult)
        act(p, fo.full, sso.full, AF.Ln, bias=1e-6)
        act(p, fo.full, fo.full, AF.Exp, scale=-0.5)
        tt(p, DVE, fo.full, fo.full, oscale[v], ALU.mult)
        tt(p, DVE, r3(ot.full, 8, 64), r3(ob.full, 8, 64), bc(fo.full, [128, 8, 64], 2), ALU.mult)
        tt(p, DVE, mix[:, 0:512], ot.full, gq.full, ALU.mult)

    def swa_out(pvb, j):
        pv3 = pvb[:, 0:264].rearrange("p (g c) -> p g c", g=4, c=66)
        tt(p, DVE, den[:, 4 * j:4 * j + 4], pv3[:, :, 64], esink[:, 4 * j:4 * j + 4], ALU.add)
        recip(p, rden[:, 4 * j:4 * j + 4], den[:, 4 * j:4 * j + 4])
        tt(p, DVE, r3(mix[:, 512 + 256 * j:768 + 256 * j], 4, 64), pv3[:, :, 0:64],
           bc(rden[:, 4 * j:4 * j + 4], [128, 4, 64], 2), ALU.mult)

    def mix_T(t):
        b = poolB.next()
        bv = b.full.bitcast(BF16)
        for k in range(8):
            tr(p, bv[:, k * 128:(k + 1) * 128], mix[:, k * 128:(k + 1) * 128], ident.full)
        cp(p, ACT, mv(t), bv.rearrange("p (k t) -> p k t", k=8))

    GB = 2
    NSL = 2
    S0f = [ar.alloc(f"S0f{i}", [64, GB * 512], F32) for i in range(NSL)]
    S0b = [ar.alloc(f"S0b{i}", [64, GB * 512], BF16) for i in range(NSL)]
    ckb = ar.alloc("ckb", [128, 16, 128], BF16)
    cvb = ar.alloc("cvb", [128, 32, 66], BF16)
    kcT = ar.alloc("kcT", [64, 32, 128], BF16)
    bm = ar.alloc("bm", [128, 16], F32)
    krm = [ar.alloc(f"krm{i}", [128, 512], BF16) for i in range(2)]
    Stmp_s = ar.alloc("Stmp_s", [64, 512], F32)
    pTc = [ar.alloc(f"pTc{j}", [128, 512], BF16) for j in range(2)]
    oTs_sb = ar.alloc("oTs_sb", [65, 8, 128], F32)
    oT_sb = oTs_sb
    dummy = p.tile(None, "dummy_d2d")

    dma(p, SP, bm.full, c_bm)
    bm.const = True
    mset(p, POOL, cvb.full, 1.0)
    dma(p, POOL, ckb.full, ck.rearrange("b k f -> k b f"))
    cv_v = cv.rearrange("b k (j d) -> k b j d", j=2)
    cvb4 = cvb.full.rearrange("k (b j) d -> k b j d", j=2)
    for j in range(2):
        dma(p, POOL, cvb4[:, :, j, 0:64], cv_v[:, :, j, :])
    dma(p, SP, View(dummy, kso[:, 0:120, :]), ck[:, 8:128, :])
    dma(p, SP, View(dummy, vso[:, 0:120, :]), cv[:, 8:128, :])

    if stop == "pre":
        return nc, p.emit(), 0
    qr, kr, vr, gq = qr_[0], kr_[0], vr_[0], gq_[0]
    qT_r, kT_r, qT_s = qT_r_[0], kT_r_[0], qT_s_[0]
    front(xs, 0, load=False)
    p.add(ACT, lambda h: h.activation(esink.full.ap, esink.full.ap, AF.Exp), reads=[esink, hnT[0]], writes=[esink], tag="act", dur=0.25)
    esink.const = True
    if stop == "front":
        return nc, p.emit(), 0
    inproj(0, 1, 0)
    if stop == "inproj":
        return nc, p.emit(), 0
    dma(p, SP, kso[:, 120:128, :], ks32.full)
    dma(p, SP, vso[:, 120:128, :], vs32.full)

    ck3 = ckb.full.rearrange("k b (j d) -> k (b j) d", d=64)
    for hf_ in range(2):
        tt(p, POOL, ck3[:, hf_ * 16:(hf_ + 1) * 16, :], ck3[:, hf_ * 16:(hf_ + 1) * 16, :], bc(qg_bc, [128, 16, 64], 1), ALU.mult)
    for r in range(4):
        b = poolB.next()
        bv = b.full.bitcast(BF16)
        for i in range(8):
            idx = r * 8 + i
            tr(p, bv[0:64, i * 128:(i + 1) * 128], ckb[:, idx // 2, (idx % 2) * 64:(idx % 2) * 64 + 64], ident.full)
        cp(p, DVE if r % 2 == 0 else ACT, kcT[:, r * 8:(r + 1) * 8, :].rearrange("p a t -> p (a t)"), bv[0:64, :])

    if stop == "kct":
        return nc, p.emit(), 0
    ret_scores(1)
    oTb = [poolC.next(), poolC.next()]
    for h in range(8):
        mm(p, oTb[h // 4][0:64, (h % 4) * 128:(h % 4 + 1) * 128], vr[:, h * 64:(h + 1) * 64], scm[:, h, :],
           start=(h % 4 == 0), stop=False, skip_group_check=True)
    s0_v = s0.rearrange("b h d e -> d (b h) e")
    rs_v = rs.rearrange("b h d e -> d (b h) e")
    for bg in range(16 // GB):
        sl = bg % NSL
        S0f4 = S0f[sl].full.rearrange("d (a e) -> d a e", e=64)
        dma(p, SP, S0f4, s0_v[:, bg * GB * 8:(bg + 1) * GB * 8, :])
        dma(p, POOL, S0b[sl].full.rearrange("d (a e) -> d a e", e=64), s0_v[:, bg * GB * 8:(bg + 1) * GB * 8, :])
        for bb in range(GB):
            b_ = bg * GB + bb
            for h in range(8):
                mm(p, oTb[h // 4][0:64, (h % 4) * 128 + b_ * 8:(h % 4) * 128 + b_ * 8 + 8],
                   S0b[sl][:, (bb * 8 + h) * 64:(bb * 8 + h + 1) * 64], qT_r[:, h, b_ * 8:(b_ + 1) * 8],
                   start=False, stop=True, skip_group_check=True)
        for bb in range(GB):
            b_ = bg * GB + bb
            km = krm[b_ % 2]
            ts(p, DVE, km.full, kr.full, bm[:, b_:b_ + 1], None, ALU.mult)
            stb = poolA.next()
            for h in range(8):
                mm(p, stb[0:64, h * 64:(h + 1) * 64], km[:, h * 64:(h + 1) * 64], vr[:, h * 64:(h + 1) * 64])
            seg = S0f[sl][:, bb * 512:(bb + 1) * 512]
            tt(p, DVE, Stmp_s.full, stb[0:64, :], seg, ALU.add)
            tt(p, DVE, r3(seg, 8, 64), r3(Stmp_s.full, 8, 64), bc(g8, [64, 8, 64], 2), ALU.mult)
        dma(p, SP, rs_v[:, bg * GB * 8:(bg + 1) * GB * 8, :], S0f4)
    for half in range(2):
        cp(p, ACT, oT_sb[0:64, half * 4:(half + 1) * 4, :].rearrange("p a t -> p (a t)"), oTb[half][0:64, :])
    ob = poolC.next()
    for h in range(8):
        tr(p, ob[:, h * 64:(h + 1) * 64], oT_sb[0:64, h, :], ident32[0:64, 0:64])
    ret_out(ob, 1)

    if stop == "sret":
        return nc, p.emit(), 0
    for j in range(2):
        b = poolC.next()
        b3 = r3(b.full, 4, 128)
        for b_ in range(16):
            mm(p, b3[:, :, b_ * 8:(b_ + 1) * 8], kcT[:, b_ * 2 + j, :], qT_s[:, 4 * j:4 * j + 4, b_ * 8:(b_ + 1) * 8])
        bp = r3(bias_prev, 8, 128)[:, 4 * j:4 * j + 4, 0:8]
        bpb = View(bp.tile, bp.ap.unsqueeze(2).to_broadcast([128, 4, 16, 8]))
        tt(p, DVE, tb[j].full.rearrange("p (g b l) -> p g b l", g=4, b=16), b.full.rearrange("p (g b l) -> p g b l", g=4, b=16), bpb, ALU.add)
        act(p, pTc[j].full, tb[j].full, AF.Exp)
    for j in range(2):
        b = poolC.next()
        mm(p, b.full, kT_s[0][:, j, :], qT_s[:, 4 * j:4 * j + 4, :].rearrange("p g t -> p (g t)"))
        tt(p, DVE, tb[j].full, b.full, bias_sn[:, 512 * j:512 * (j + 1)], ALU.add)
        act(p, pT[j][0].full, tb[j].full, AF.Exp)
    for j in range(2):
        b = poolC.next()
        mm(p, b[0:65, :], vaug[0][:, j, 0:65], pT[j][0].full, start=True, stop=False, skip_group_check=True)
        b3 = r3(b.full, 4, 128)
        pc3 = r3(pTc[j].full, 4, 128)
        for b_ in range(16):
            mm(p, b3[0:65, :, b_ * 8:(b_ + 1) * 8], cvb[:, b_ * 2 + j, 0:65], pc3[:, :, b_ * 8:(b_ + 1) * 8],
               start=False, stop=True, skip_group_check=True)
        cp(p, ACT, oTs_sb[:, 4 * j:4 * j + 4, :].rearrange("p a t -> p (a t)"), b[0:65, :])
    for j in range(2):
        pvb = poolC.next()
        for g in range(4):
            tr(p, pvb[:, g * 66:g * 66 + 65], oTs_sb[:, 4 * j + g, :], ident32[0:65, 0:65])
        swa_out(pvb, j)
    mix_T(16)

    if stop == "S":
        return nc, p.emit(), 0
    ar.reset(mark_sa)
    S32 = ar.alloc("S32", [64, 512], F32)
    Sbf = ar.alloc("Sbf", [64, 512], BF16)
    Stmp = ar.alloc("Stmp", [64, 512], F32)
    mset(p, POOL, S32.full, 0.0)
    rp_v = rp.rearrange("h d e -> d h e")
    hn32T = ar.alloc("hn32T", [128, 8, 128], F32)
    w32 = [ar.alloc(f"w32_{i}", [128, 1024], F32) for i in range(2)]
    def xload(t):
        dma(p, SP, xt[t % 2].full, xp[t * 128:(t + 1) * 128, :])

    def tok0_path():
        for half in range(2):
            b = poolC.next()
            for kk in range(4):
                k = half * 4 + kk
                tr(p, b[:, kk * 128:(kk + 1) * 128], xt[1][:, k * 128:(k + 1) * 128], ident32.full)
            cp(p, ACT if half == 0 else DVE, hn32T[:, half * 4:(half + 1) * 4, :].rearrange("p a t -> p (a t)"), b.full)
        bq, bk = poolC.next(), poolC.next()
        for k in range(8):
            dma(p, SP, w32[k % 2].full, w_in[k * 128:(k + 1) * 128, 0:1024])
            mm(p, bq.full, hn32T[:, k, :], w32[k % 2][:, 0:512], start=(k == 0), stop=(k == 7))
            mm(p, bk.full, hn32T[:, k, :], w32[k % 2][:, 512:1024], start=(k == 0), stop=(k == 7))
        q32sb, k32sb = w32[0][:, 0:512], w32[0][:, 512:1024]
        cp(p, ACT, q32sb, bq.full)
        tt(p, DVE, r3(k32sb, 8, 64), r3(bk.full, 8, 64), bc(kscale[0], [128, 8, 64], 2), ALU.mult)
        for src, dst, eng in ((q32sb, hn32T, ACT), (k32sb, w32[1], DVE)):
            for half in range(2):
                b = poolC.next()
                for hh in range(4):
                    h = half * 4 + hh
                    tr(p, b[0:64, hh * 128:(hh + 1) * 128], src[:, h * 64:(h + 1) * 64], ident32.full)
                d2 = dst[0:64].rearrange("p a t -> p (a t)") if dst is hn32T else dst[0:64, :]
                cp(p, eng, d2[:, half * 512:(half + 1) * 512], b[0:64, :])

    def front_stages(t):
        s = t % 2
        if t == 0:
            def F():
                front(None, 0, hn32=xt[1], load=False)
                tok0_path()
        else:
            def F():
                front(None, s, load=False)
        st = [F] + inproj_stages(s, 0, t % 3)
        if t == NT_P - 1:
            last = st[-1]

            def T1x():
                last()
                dma(p, SP, kp, ks32.full)
                dma(p, SP, vp, vs32.full)
            st[-1] = T1x
        return st

    def back_stages(t):
        s = t % 2
        v3, v3p = t % 3, (t - 1) % 3
        qT_r, kr, vr, qT_s = qT_r_[s], kr_[s], vr_[s], qT_s_[s]
        hold = {}

        def R1():
            if t == 0:
                qT32 = hn32T[0:64]
                kT32 = w32[1][0:64, :].rearrange("p (a t) -> p a t", a=8)
                for half in range(2):
                    b = poolC.next()
                    for hh in range(4):
                        h = half * 4 + hh
                        mm(p, b[:, hh * 128:(hh + 1) * 128], kT32[:, h, :], qT32[:, h, :])
                    tt(p, DVE, scm[:, half * 4:(half + 1) * 4, :], r3(b.full, 4, 128), bc(cm[0], [128, 4, 128], 1), ALU.mult)
            else:
                ret_scores(0, s)

        def R2():
            ob = poolC.next()
            hold["ob"] = ob
            for h in range(8):
                sl_ = slice(h * 64, (h + 1) * 64)
                mm(p, ob[:, sl_], scm[:, h, :], vr[:, sl_], start=True, stop=(t == 0))
                if t > 0:
                    mm(p, ob[:, sl_], qT_r[:, h, :], Sbf[:, sl_], start=False, stop=True)
            stb = poolC.next()
            hold["stb"] = stb
            for h in range(8):
                sl_ = slice(h * 64, (h + 1) * 64)
                mm(p, stb[0:64, sl_], kr[:, sl_], vr[:, sl_])

        def R3():
            stb = hold["stb"]
            tt(p, DVE, Stmp.full, stb[0:64, :], S32.full, ALU.add)
            tt(p, POOL, r3(S32.full, 8, 64), r3(Stmp.full, 8, 64), bc(g128, [64, 8, 64], 2), ALU.mult)
            if t < NT_P - 1:
                cp(p, POOL, Sbf.full, S32.full)
            else:
                dma(p, SP, rp_v, r3(S32.full, 8, 64))
            ret_out(hold["ob"], 0, s)

        blks = ([(v3p, bias_prev, 0)] if t > 0 else []) + [(v3, bias_cur, 1)]

        def W1():
            for j in range(2):
                for (ksl, bias_v, bi) in blks:
                    b = poolC.next()
                    mm(p, b.full, kT_s[ksl][:, j, :], qT_s[:, 4 * j:4 * j + 4, :].rearrange("p g t -> p (g t)"))
                    tt(p, DVE, tb[bi].full, b.full, bias_v[:, 512 * j:512 * (j + 1)], ALU.add)
                    act(p, pT[j][bi].full, tb[bi].full, AF.Exp)

        def W2():
            for j in range(2):
                pvb = poolC.next()
                for g in range(4):
                    for n_, (ksl, bias_v, bi) in enumerate(blks):
                        mm(p, pvb[:, g * 66:g * 66 + 65], pT[j][bi][:, g * 128:(g + 1) * 128], vaug[ksl][:, j, 0:65],
                           start=(n_ == 0), stop=(n_ == len(blks) - 1))
                swa_out(pvb, j)

        def M():
            if debug and t == 0:
                dma(p, POOL, dbg_mix, mix.full)
                dma(p, SP, dbg_ot, ot.full)
                dma(p, SP, dbg_gq, gq_[0].full)
            mix_T(t)

        return [R1, R2, R3, W1, W2, M]

    xload(0)
    for i in range(-1, nA):
        fs = front_stages(i + 1) if i + 1 < nA else []
        bs = back_stages(i) if i >= 0 else []
        order = [("f", 0), ("x", 0), ("b", 0), ("f", 1), ("b", 1), ("f", 2), ("b", 2), ("b", 3), ("f", 3), ("b", 4), ("f", 4), ("b", 5)]
        for kind, idx in order:
            if kind == "x":
                if i + 2 < nA:
                    xload(i + 2)
                continue
            lst = fs if kind == "f" else bs
            if idx < len(lst):
                lst[idx]()

    if stop == "A":
        return nc, p.emit(), 0
    ar.reset((ar.base, 0))
    p.sync_lat = SYNC_B
    p.pe_scale = 1.0
    w_out_t = [ar.alloc(f"w_out{h}", [128, 8, 512], BF16) for h in range(2)]
    w_up_t = [ar.alloc(f"w_up{g}", [128, 8, 512], BF16) for g in range(8)]
    w_dn_t = [ar.alloc(f"w_dn{g}", [128, 4, D], BF16) for g in range(8)]
    xh = [ar.alloc(f"xh{i}", [128, 2, D], F32) for i in range(2)]
    smB = ar.alloc("smB", [128, 4], F32)
    hn2 = ar.alloc("hn2", [128, D], BF16)
    r1 = [ar.alloc(f"r1{i}", [128, 256], F32) for i in range(3)]
    a_t = [ar.alloc(f"a{i}", [128, 256], BF16) for i in range(4)]

    for h in range(2):
        dma(p, POOL, w_out_t[h].full, w_out_v[:, :, h * 512:(h + 1) * 512])
    w_up_v = w_up.rearrange("(k p) n -> p k n", p=128)
    w_dn_v = w_down.rearrange("(c p) n -> p c n", p=128)
    for g in range(8):
        dma(p, POOL, w_up_t[g].full, w_up_v[:, :, g * 512:(g + 1) * 512])
        dma(p, POOL, w_dn_t[g].full, w_dn_v[:, g * 4:(g + 1) * 4, :])
    for t_ in w_out_t + w_up_t + w_dn_t:
        t_.const = True

    if stop == "Bw":
        return nc, p.emit(), 0
    supers = [[2 * i, 2 * i + 1] for i in range(8)] + [[16]]
    if stop in ("Bp", "B1", "Bf"):
        supers = supers[:1]
    yb = [[banks[0], banks[1]], [banks[2], banks[3]]]
    pbank = banks[4]
    ubk = [banks[5], banks[6], banks[7]]

    def prologue_stages(si):
        X = xh[si % 2]
        st = []
        for i, t in enumerate(supers[si]):
            def A(i=i, t=t):
                x_ap = xs if t == 16 else xp[t * 128:(t + 1) * 128, :]
                dma(p, SP, X[:, i, :], x_ap)
                for half in range(2):
                    for k in range(8):
                        mm(p, pbank.full, mv(t)[:, k, :], w_out_t[half][:, k, :], start=(k == 0), stop=(k == 7))
                    tt(p, DVE, X[:, i, half * 512:(half + 1) * 512], pbank.full, X[:, i, half * 512:(half + 1) * 512], ALU.add)
                act(p, hn2[:, 0:512], X[:, i, 0:512], AF.Square, scale=1.0 / 32.0, accum_out=smB[:, 0:1])
                act(p, hn2[:, 512:1024], X[:, i, 512:1024], AF.Square, scale=1.0 / 32.0, accum_out=smB[:, 3:4])
                tt(p, DVE, smB[:, 0:1], smB[:, 0:1], smB[:, 3:4], ALU.add)
                act(p, smB[:, 1:2], smB[:, 0:1], AF.Ln, bias=1e-6)
                act(p, smB[:, 2:3], smB[:, 1:2], AF.Exp, scale=-0.5)
                for hh_ in range(2):
                    cs = slice(hh_ * 512, (hh_ + 1) * 512)
                    stt(p, DVE, hn2[:, cs], X[:, i, cs], smB[:, 2:3], gffn[:, cs], ALU.mult, ALU.mult)

            def C(i=i, t=t):
                bv = pbank.full.bitcast(BF16)
                for k in range(8):
                    tr(p, bv[:, k * 128:(k + 1) * 128], hn2[:, k * 128:(k + 1) * 128], ident.full)
                bv3 = bv.rearrange("p (k t) -> p k t", k=8)
                cp(p, ACT, mv(t)[:, 0:4, :], bv3[:, 0:4, :])
                cp(p, ACT, mv(t)[:, 4:8, :], bv3[:, 4:8, :])
            st += [A, C]
        return st

    def prologue(si):
        for f_ in prologue_stages(si):
            f_()

    def ffn(si, hooks=None):
        hooks = hooks or {}
        tl = supers[si]
        TS = len(tl)
        NTOK = TS * 128
        Hp = mixTp[si]

        def up(c):
            ub = ubk[c % 3][:, 0:NTOK]
            for k in range(8):
                mm(p, ub, w_up_t[c // 4][:, k, (c % 4) * 128:(c % 4 + 1) * 128], Hp[:, 0:TS, k, :], start=(k == 0), stop=(k == 7))

        up(0)
        up(1)
        for c in range(32):
            if c in hooks:
                hooks[c]()
            if c + 2 < 32:
                up(c + 2)
            ub = ubk[c % 3][:, 0:NTOK]
            r_ = r1[c % 3][:, 0:NTOK]
            a_ = a_t[c % 4][:, 0:NTOK]
            act(p, r_, ub, AF.Relu)
            tt(p, DVE, a_, r_, r_, ALU.mult)
            for i in range(TS):
                for half in range(2):
                    mm(p, yb[i][half].full, a_t[c % 4][:, i * 128:(i + 1) * 128], w_dn_t[c // 4][:, c % 4, half * 512:(half + 1) * 512],
                       start=(c == 0), stop=(c == 31))

    def epilogue(si):
        X = xh[si % 2]
        for i, t in enumerate(supers[si]):
            for half in range(2):
                tt(p, DVE, X[:, i, half * 512:(half + 1) * 512], yb[i][half].full, X[:, i, half * 512:(half + 1) * 512], ALU.add)
            y_ap = ys if t == 16 else yp[t * 128:(t + 1) * 128, :]
            dma(p, SP, y_ap, X[:, i, :])

    prologue(0)
    if stop == "Bp":
        return nc, p.emit(), 0
    p.pin = PIN_B
    for si in range(len(supers)):
        hooks = {}
        if si + 1 < len(supers):
            st = prologue_stages(si + 1)
            for c_, f_ in zip((1, 8, 15, 22), st):
                hooks[c_] = f_
        ffn(si, hooks)
        epilogue(si)
    p.pin = False

    stats = p.emit()
    return nc, stats, (pers.cur, ar.peak, top)


def _consts():
    f = np.float32
    H = 8
    gam = 1.0 - 2.0 ** (-5.0 - np.arange(H, dtype=np.float64))
    slopes = 2.0 ** (-8.0 * np.arange(1, H + 1, dtype=np.float64) / H)
    idx = np.arange(128)
    c_id = np.eye(128, dtype=f)
    cmask_p = (idx[None, :] >= idx[:, None]).astype(f)
    seq = idx // 8
    pos = idx % 8
    cmask_s = ((seq[None, :] == seq[:, None]) & (pos[None, :] >= pos[:, None])).astype(f)
    c_mask = np.concatenate([cmask_p, cmask_s], axis=1)

    def sc(posv):
        ks = gam[None, :] ** (-(posv[:, None] + 1.0)) * 64.0 ** -0.5
        os_ = gam[None, :] ** (posv[:, None] + 1.0)
        return [ks, os_, os_ * os_ / 64.0]

    c_sc = np.concatenate(sc(idx.astype(np.float64)) + sc(pos.astype(np.float64)), axis=1).astype(f)
    c_g = np.concatenate([np.broadcast_to(gam[None, :] ** 128.0, (64, 8)), np.broadcast_to(gam[None, :] ** 8.0, (64, 8))], axis=1).astype(f)
    NEG = -30000.0
    key = idx[:, None, None]
    q = idx[None, None, :]
    sl = slopes[None, :, None]
    b_cur = np.where(key <= q, -sl * (q - key), NEG)
    b_prev = np.where(key > q, -sl * (128 + q - key), NEG)
    kseq, kpos = seq[:, None, None], pos[:, None, None]
    qseq, qpos = seq[None, None, :], pos[None, None, :]
    b_sn = np.where((kseq == qseq) & (kpos <= qpos), -sl * (qpos - kpos), NEG)
    c_bias = np.concatenate([b_cur.reshape(128, -1), b_prev.reshape(128, -1), b_sn.reshape(128, -1)], axis=1).astype(f)
    c_bm = (seq[:, None] == np.arange(16)[None, :]).astype(f)
    return dict(c_id=c_id, c_mask=c_mask, c_sc=c_sc, c_g=c_g, c_bias=c_bias, c_bm=c_bm)


_CACHE = {}


def kernel(x_prompt, x_sample, state_ret, cache_swa_k, cache_swa_v, norm_mix_gain, w_in,
           q_norm_gain, k_norm_gain, attn_sinks, w_out, norm_ffn_gain, w_up, w_down):
    f = np.float32
    A = lambda a: np.ascontiguousarray(np.asarray(a), dtype=f)
    if "nc" not in _CACHE:
        _CACHE["nc"] = build_program()[0]
    nc = _CACHE["nc"]
    cst = _consts()
    rep = lambda v_: np.ascontiguousarray(np.broadcast_to(A(v_)[None, :], (128, A(v_).shape[0])))
    shared = dict(w_in=A(w_in), w_out=A(w_out), w_up=A(w_up), w_down=A(w_down),
                  v_gmix=rep(norm_mix_gain), v_gffn=rep(norm_ffn_gain),
                  v_qk=np.ascontiguousarray(np.concatenate([rep(q_norm_gain), rep(k_norm_gain)], axis=1)),
                  v_sink=rep(attn_sinks), **cst)
    xp_, xs_, s0_, ck_, cv_ = A(x_prompt), A(x_sample), A(state_ret), A(cache_swa_k), A(cache_swa_v)
    in_maps = []
    for c in range(NCORES):
        m = dict(shared)
        m["xp"] = xp_[c]
        m["xs"] = xs_[16 * c:16 * c + 16].reshape(128, D)
        m["s0"] = s0_[16 * c:16 * c + 16]
        m["ck"] = ck_[16 * c:16 * c + 16].reshape(16, 128, 128)
        m["cv"] = cv_[16 * c:16 * c + 16].reshape(16, 128, 128)
        in_maps.append(m)
    res = run_bass_kernel_spmd(nc, in_maps, core_ids=list(range(NCORES)))
    R = res.results
    y_prompt = np.stack([R[c]["yp"] for c in range(NCORES)]).reshape(8, 2048, D)
    y_sample = np.concatenate([R[c]["ys"].reshape(16, 8, D) for c in range(NCORES)])
    ret_p = np.stack([R[c]["rp"] for c in range(NCORES)]).reshape(8, 8, 64, 64)
    k_p = np.stack([R[c]["kp"] for c in range(NCORES)]).reshape(8, 128, 2, 64)
    v_p = np.stack([R[c]["vp"] for c in range(NCORES)]).reshape(8, 128, 2, 64)
    ret_s = np.concatenate([R[c]["rs"] for c in range(NCORES)]).reshape(128, 8, 64, 64)
    k_s = np.concatenate([R[c]["kso"] for c in range(NCORES)]).reshape(128, 128, 2, 64)
    v_s = np.concatenate([R[c]["vso"] for c in range(NCORES)]).reshape(128, 128, 2, 64)
    return tuple(np.asarray(a, dtype=f) for a in (y_prompt, y_sample, ret_p, k_p, v_p, ret_s, k_s, v_s))
```

```python
import numpy as np
import concourse.bass as bass
import concourse.mybir as mybir

F32 = mybir.dt.float32
BF16 = mybir.dt.bfloat16
AF = mybir.ActivationFunctionType
ALU = mybir.AluOpType
AX = mybir.AxisListType

SAME_ENGINE_WAR_SEM = False
PE, ACT, DVE, POOL, SP = "pe", "act", "dve", "pool", "sp"
COMPUTE = (PE, ACT, DVE, POOL)
ENGS = (PE, ACT, DVE, POOL, SP)


class Tile:
    def __init__(self, prog, handle, name):
        self.prog = prog
        self.h = handle
        self.name = name
        self.last_w = []
        self.readers = []
        self.last_acc = {}
        self.const = False
        self.excl = False
        self.sem = None
        self.ndma = 0

    def __getitem__(self, key):
        return View(self, self.h.ap()[key])

    @property
    def full(self):
        return View(self, self.h.ap())


class View:
    def __init__(self, tile, ap):
        self.tile = tile
        self.ap = ap

    def __getitem__(self, key):
        return View(self.tile, self.ap[key])

    def bitcast(self, dt):
        return View(self.tile, self.ap.bitcast(dt))

    def rearrange(self, s, **kw):
        return View(self.tile, self.ap.rearrange(s, **kw))


class Op:
    __slots__ = ("eng", "lidx", "fn", "preds", "raw", "signal", "is_dma", "sem", "target", "tag",
                 "dur", "lat", "pos", "fin", "nsucc", "sync", "cyc")

    def __init__(self, eng, lidx, fn, tag=""):
        self.eng = eng
        self.lidx = lidx
        self.fn = fn
        self.preds = set()
        self.raw = set()
        self.signal = False
        self.is_dma = False
        self.sem = None
        self.target = 0
        self.tag = tag
        self.dur = 0.1
        self.lat = 0.0
        self.pos = -1
        self.fin = 0.0
        self.cyc = 0.0


class Prog:
    def __init__(self, nc, schedule=True, window=200):
        self.nc = nc
        self.all = []
        self.tiles = []
        self.schedule = schedule
        self.window = window
        self.pin = False
        self.last_pin = {}
        self.sync_lat = 1.2
        self.pe_scale = 1.0
        self.pace = False
        self.pace_hi = 0.55
        self.pace_lo = 0.42

    def tile(self, handle, name):
        t = Tile(self, handle, name)
        self.tiles.append(t)
        return t

    def add(self, eng, fn, reads=(), writes=(), dma=False, tag="", dur=0.1, lat=0.0, cyc=0.0):
        op = Op(eng, len(self.all), fn, tag)
        op.cyc = cyc
        op.is_dma = dma
        op.dur = dur * (self.pe_scale if eng == PE else 1.0)
        op.lat = lat
        op.sync = self.sync_lat
        rt, wt = [], []
        for x in reads:
            t = x.tile if isinstance(x, View) else x
            if t is not None and t not in rt:
                rt.append(t)
        for x in writes:
            t = x.tile if isinstance(x, View) else x
            if t is not None and t not in wt:
                wt.append(t)
        for t in rt:
            if t in wt:
                continue
            for w in t.last_w:
                op.preds.add(w)
                op.raw.add(w)
            if t.excl:
                for r in t.readers:
                    if r.eng != eng:
                        op.preds.add(r)
        for t in wt:
            for w in t.last_w:
                op.preds.add(w)
                if t in rt or SAME_ENGINE_WAR_SEM:
                    op.raw.add(w)
            for r in t.readers:
                op.preds.add(r)
                if SAME_ENGINE_WAR_SEM:
                    op.raw.add(r)
        for t in rt + wt:
            if not t.const:
                prev = t.last_acc.get(eng)
                if prev is not None:
                    op.preds.add(prev)
                t.last_acc[eng] = op
        if self.pin:
            prev = self.last_pin.get(eng)
            if prev is not None:
                op.preds.add(prev)
            self.last_pin[eng] = op
        op.preds.discard(op)
        if dma:
            owner = None
            for t in wt + rt:
                if t.h is not None:
                    owner = t
                    break
            if owner is None:
                owner = (wt + rt)[0]
            op.sem = owner
            owner.ndma += 1
            op.target = 16 * owner.ndma
        for t in rt:
            if t not in wt:
                t.readers.append(op)
        for t in wt:
            t.last_w = [op]
            t.readers = []
        self.all.append(op)
        return op

    @staticmethod
    def _needs_sem(op, q):
        if q.is_dma:
            return True
        if q.eng != op.eng or op.is_dma:
            return True
        return (q in op.raw) and op.eng != PE

    def _schedule(self):
        order = {e: [] for e in ENGS}
        if not self.schedule:
            for op in self.all:
                op.pos = len(order[op.eng])
                order[op.eng].append(op)
            return order
        succs = {}
        npend = {}
        ready = {}
        for op in self.all:
            npend[op] = len(op.preds)
            ready[op] = 0.0
            for q in sorted(op.preds, key=lambda o: o.lidx):
                succs.setdefault(q, []).append(op)
        tail = {}
        for op in reversed(self.all):
            t_ = 0.0
            for s_ in succs.get(op, ()):
                v = tail[s_] + (s_.sync if self._needs_sem(s_, op) else 0.0)
                if v > t_:
                    t_ = v
            tail[op] = t_ + op.dur + op.lat
        pend = {e: [op for op in self.all if op.eng == e] for e in ENGS}
        head = {e: 0 for e in ENGS}
        t_free = {e: 0.0 for e in ENGS}
        nleft = len(self.all)
        dma_free = 0.0
        hist = []
        PACE_W = 3.4
        while nleft:
            best, best_t, best_i = None, None, None
            for e in ENGS:
                lst = pend[e]
                h = head[e]
                while h < len(lst) and lst[h] is None:
                    h += 1
                head[e] = h
                cnt = 0
                i = h
                tf = t_free[e]
                pe_cand = None
                while i < len(lst) and cnt < self.window:
                    op = lst[i]
                    i += 1
                    if op is None:
                        continue
                    cnt += 1
                    if npend[op] > 0:
                        continue
                    st = ready[op] if ready[op] > tf else tf
                    if e == PE and self.pace and st <= tf + 1e-9:
                        if pe_cand is None:
                            pe_cand = []
                        pe_cand.append((op, st, i - 1))
                        continue
                    if best is None or st < best_t - 1e-9 or (abs(st - best_t) <= 1e-9 and tail[op] > tail[best]):
                        best, best_t, best_i = op, st, i - 1
                if e == PE and pe_cand:
                    while hist and hist[0][0] < tf - PACE_W:
                        hist.pop(0)
                    ratio = sum(c for _, c in hist) / (PACE_W * 2400.0)
                    if ratio > self.pace_hi:
                        pick = min(pe_cand, key=lambda x: (x[0].cyc / max(x[0].dur, 1e-3), -tail[x[0]]))
                    elif ratio < self.pace_lo:
                        pick = max(pe_cand, key=lambda x: (x[0].cyc / max(x[0].dur, 1e-3), tail[x[0]]))
                    else:
                        pick = max(pe_cand, key=lambda x: tail[x[0]])
                    op, st, ii = pick
                    if best is None or st < best_t - 1e-9 or (abs(st - best_t) <= 1e-9 and tail[op] > tail[best]):
                        best, best_t, best_i = op, st, ii
            assert best is not None, "scheduler deadlock"
            e = best.eng
            best.pos = len(order[e])
            order[e].append(best)
            t_free[e] = best_t + best.dur
            best.fin = best_t + best.dur + best.lat
            if e == PE:
                hist.append((t_free[e], best.cyc))
            pend[e][best_i] = None
            nleft -= 1
            for s_ in succs.get(best, ()):
                npend[s_] -= 1
                r = best.fin + (s_.sync if self._needs_sem(s_, best) else 0.0)
                if r > ready[s_]:
                    ready[s_] = r
        self.est = max(t_free.values())
        return order

    def emit(self):
        nc = self.nc
        order = self._schedule()
        for op in self.all:
            for q in op.preds:
                if q.is_dma:
                    continue
                if self._needs_sem(op, q):
                    q.signal = True
                else:
                    assert q.pos < op.pos, ("same-engine order violated", q.tag, op.tag)
        cnt = {}
        for e in ENGS:
            c = 0
            arr = []
            for op in order[e]:
                if op.signal and not op.is_dma:
                    c += 1
                arr.append(c)
            cnt[e] = arr
        sems = {e: nc.alloc_semaphore(name=f"s_{e}") for e in ENGS}
        for t in self.tiles:
            if t.ndma > 0:
                t.sem = nc.alloc_semaphore(name=f"d_{t.name}")
        stats = {e: [0, 0] for e in ENGS}
        with nc.Block() as block:
            def run(e):
                def body(h):
                    seen = {}
                    seen_d = {}
                    for op in order[e]:
                        need = {}
                        for q in sorted(op.preds, key=lambda o: o.lidx):
                            if q.is_dma:
                                k = id(q.sem)
                                if seen_d.get(k, 0) < q.target:
                                    h.wait_ge(q.sem.sem, q.target)
                                    seen_d[k] = q.target
                                    stats[e][1] += 1
                            elif self._needs_sem(op, q):
                                v = cnt[q.eng][q.pos]
                                if need.get(q.eng, 0) < v:
                                    need[q.eng] = v
                        for de, v in need.items():
                            if seen.get(de, 0) < v:
                                h.wait_ge(sems[de], v)
                                seen[de] = v
                                stats[e][1] += 1
                        ins = op.fn(h)
                        stats[e][0] += 1
                        if op.is_dma:
                            ins.then_inc(op.sem.sem, 16)
                        elif op.signal:
                            ins.then_inc(sems[e], 1)
                    for op in order[e]:
                        if op.is_dma:
                            k = id(op.sem)
                            if seen_d.get(k, 0) < op.target:
                                h.wait_ge(op.sem.sem, op.target)
                                seen_d[k] = op.target
                return body

            block.tensor(run(PE))
            block.scalar(run(ACT))
            block.vector(run(DVE))
            block.gpsimd(run(POOL))
            block.sync(run(SP))
        self.stats = stats
        return stats


def _ap(x):
    return x.ap if isinstance(x, View) else x


def _views(*xs):
    return [x for x in xs if isinstance(x, View)]


def _fd(x):
    a = _ap(x)
    n = 1
    for d in a.shape[1:]:
        n *= d
    return n


def _cost(eng, fd, f32=True):
    if eng == ACT:
        return 0.22 + fd * 0.00083
    if eng == DVE:
        return 0.08 + fd * (0.00115 if f32 else 0.0007)
    if eng == POOL:
        return 0.25 + fd * 0.0035
    return 0.1


def mm(p, out, lhsT, rhs, start=True, stop=True, **kw):
    n = _fd(rhs)
    f32 = _ap(rhs).dtype == F32
    d = (max(n, 64) / 2400.0 + 0.03) * (4.0 if f32 else 1.0)
    return p.add(PE, lambda h: h.matmul(_ap(out), _ap(lhsT), _ap(rhs), start=start, stop=stop, **kw),
                 reads=_views(lhsT, rhs), writes=_views(out), tag="mm", dur=d, lat=0.25,
                 cyc=n * (4.0 if f32 else 1.0))


def tr(p, out, in_, ident):
    f32 = _ap(in_).dtype == F32
    d = 0.12 * (4.0 if f32 else 1.0)
    return p.add(PE, lambda h: h.transpose(_ap(out), _ap(in_), _ap(ident)),
                 reads=_views(in_, ident), writes=_views(out), tag="tr", dur=d, lat=0.25, cyc=128.0)


def act(p, out, in_, func, bias=None, scale=None, accum_out=None, eng=ACT):
    kw = {}
    if bias is not None:
        kw["bias"] = _ap(bias)
    if scale is not None:
        kw["scale"] = _ap(scale)
    if accum_out is not None:
        kw["accum_out"] = _ap(accum_out)
    return p.add(eng, lambda h: h.activation(_ap(out), _ap(in_), func, **kw),
                 reads=_views(in_, bias, scale), writes=_views(out, accum_out), tag="act", dur=_cost(ACT, _fd(out)))


def tt(p, eng, out, in0, in1, op):
    return p.add(eng, lambda h: h.tensor_tensor(_ap(out), _ap(in0), _ap(in1), op),
                 reads=_views(in0, in1), writes=_views(out), tag="tt", dur=_cost(eng, _fd(out)))


def ts(p, eng, out, in0, s1, s2, op0, op1=None, accum_out=None):
    kw = {}
    if op1 is not None:
        kw["op1"] = op1
    if accum_out is not None:
        kw["accum_out"] = _ap(accum_out)
    return p.add(eng, lambda h: h.tensor_scalar(_ap(out), _ap(in0), _ap(s1), _ap(s2) if s2 is not None else None, op0, **kw),
                 reads=_views(in0, s1, s2), writes=_views(out, accum_out), tag="ts", dur=_cost(eng, _fd(out), False))


def stt(p, eng, out, in0, scalar, in1, op0, op1):
    return p.add(eng, lambda h: h.scalar_tensor_tensor(_ap(out), _ap(in0), _ap(scalar), _ap(in1), op0, op1),
                 reads=_views(in0, scalar, in1), writes=_views(out), tag="stt", dur=_cost(eng, _fd(out)))


def cp(p, eng, out, in_):
    d = _cost(eng, _fd(out), _ap(in_).dtype == F32)
    if eng == ACT:
        return p.add(eng, lambda h: h.copy(_ap(out), _ap(in_)), reads=_views(in_), writes=_views(out), tag="cp", dur=d)
    return p.add(eng, lambda h: h.tensor_copy(_ap(out), _ap(in_)), reads=_views(in_), writes=_views(out), tag="cp", dur=d)


def red(p, eng, out, in_, op=None, axis=None):
    op = op or ALU.add
    axis = axis or AX.X
    return p.add(eng, lambda h: h.tensor_reduce(_ap(out), _ap(in_), axis, op),
                 reads=_views(in_), writes=_views(out), tag="red", dur=_cost(eng, _fd(in_)))


def mset(p, eng, out, val):
    return p.add(eng, lambda h: h.memset(_ap(out), val), reads=(), writes=_views(out), tag="mset", dur=_cost(eng, _fd(out), False) * 0.5)


def recip(p, out, in_):
    return p.add(DVE, lambda h: h.reciprocal(_ap(out), _ap(in_)), reads=_views(in_), writes=_views(out), tag="rcp",
                 dur=0.1 + _fd(out) * 0.0065)


def dma(p, eng, out, in_, **kw):
    a = _ap(out)
    nbytes = 1
    for d in a.shape:
        nbytes *= d
    nbytes *= 4
    issue = 1.1 if eng == POOL else 0.12
    return p.add(eng, lambda h: h.dma_start(out=_ap(out), in_=_ap(in_), **kw),
                 reads=_views(in_), writes=_views(out), dma=True, tag="dma", dur=issue, lat=2.0 + nbytes / 150e3)


from concourse.bass_utils import run_bass_kernel_spmd

NCORES = 8
PACE = False
SYNC_SA = 1.2
SYNC_B = 1.2
PIN_B = False
D = 1024
NT_P = 16
NTILES = 17
INW = 2816
DFF = 4096
CG = [(0, 512), (512, 1024), (1024, 1536), (1536, 2048), (2048, 2560), (2560, 2816)]


def _dsize(dt):
    return 4 if dt == F32 else 2


class Arena:
    def __init__(self, nc, prog, base, top):
        self.nc, self.p, self.base, self.top, self.cur = nc, prog, base, top, base
        self.live = []
        self.dead = []
        self.peak = base

    def alloc(self, name, shape, dt):
        nb = int(np.prod(shape[1:])) * _dsize(dt)
        off = (self.cur + 31) // 32 * 32
        assert off + nb <= self.top, f"SBUF arena overflow at {name}: {off + nb - self.top} B over"
        h = self.nc.alloc_sbuf_tensor_at(name, list(shape), dt, offset=off)
        self.cur = off + nb
        self.peak = max(self.peak, self.cur)
        t = self.p.tile(h, name)
        t.off, t.nb = off, nb
        seen_ = set()
        for (o_, n_, ops_) in self.dead:
            if o_ < off + nb and off < o_ + n_:
                for q in ops_:
                    if id(q) not in seen_:
                        seen_.add(id(q))
                        t.readers.append(q)
        self.live.append(t)
        return t

    def mark(self):
        return (self.cur, len(self.live))

    def reset(self, mark):
        cur, n = mark
        for t in self.live[n:]:
            self.dead.append((t.off, t.nb, list(t.last_w) + list(t.readers)))
        del self.live[n:]
        self.cur = cur


def bc(view, shape, axis):
    return View(view.tile, view.ap.unsqueeze(axis).to_broadcast(list(shape)))


def build_program(debug=False, stop=None, nA=NT_P, skipS=False, schedule=True, window=200):
    nc = bass.Bass("TRN2", target_bir_lowering=False)
    p = Prog(nc, schedule=schedule, window=window)
    p.sync_lat = SYNC_SA
    p.pe_scale = 1.3
    p.pace = PACE

    def din(name, shape, dt=F32):
        return nc.dram_tensor(name, list(shape), dt, kind="ExternalInput").ap()

    def dout(name, shape):
        return nc.dram_tensor(name, list(shape), F32, kind="ExternalOutput").ap()

    xp = din("xp", [2048, D]); xs = din("xs", [128, D])
    s0 = din("s0", [16, 8, 64, 64]); ck = din("ck", [16, 128, 128]); cv = din("cv", [16, 128, 128])
    w_in = din("w_in", [D, INW]); w_out = din("w_out", [D, D]); w_up = din("w_up", [D, DFF]); w_down = din("w_down", [DFF, D])
    v_gmix = din("v_gmix", [128, D]); v_gffn = din("v_gffn", [128, D])
    v_qk = din("v_qk", [128, 128]); v_sink = din("v_sink", [128, 8])
    c_id = din("c_id", [128, 128]); c_mask = din("c_mask", [128, 256]); c_sc = din("c_sc", [128, 48])
    c_g = din("c_g", [64, 16]); c_bias = din("c_bias", [128, 3 * 8 * 128]); c_bm = din("c_bm", [128, 16])
    yp = dout("yp", [2048, D]); ys = dout("ys", [128, D])
    rp = dout("rp", [8, 64, 64]); kp = dout("kp", [128, 128]); vp = dout("vp", [128, 128])
    rs = dout("rs", [16, 8, 64, 64]); kso = dout("kso", [16, 128, 128]); vso = dout("vso", [16, 128, 128])
    if debug:
        dbg_mix = dout("dbg_mix", [128, D]); dbg_ot = dout("dbg_ot", [128, 512]); dbg_gq = dout("dbg_gq", [128, 512])

    base = (nc.sbuf_base + 31) // 32 * 32
    top = nc.sbuf_top
    pers = Arena(nc, p, base, top)
    ident = pers.alloc("ident", [128, 128], BF16)
    ident32 = pers.alloc("ident32", [128, 128], F32)
    gffn = pers.alloc("gffn", [128, D], F32)
    mixTp = [pers.alloc(f"mixTp{i}", [128, 2 if i < 8 else 1, 8, 128], BF16) for i in range(9)]

    def mv(t):
        return mixTp[t // 2][:, t % 2]
    ar = Arena(nc, p, pers.cur, top)

    banks = [p.tile(nc.alloc_psum_tensor(f"bank{i}", [128, 512], F32), f"bank{i}") for i in range(8)]
    for b_ in banks:
        b_.excl = True

    class Rot:
        def __init__(self, idx):
            self.idx, self.i = idx, 0

        def next(self):
            b = banks[self.idx[self.i % len(self.idx)]]
            self.i += 1
            return b

    w_in_t = [ar.alloc(f"w_in{g}", [128, 8, c1 - c0], BF16) for g, (c0, c1) in enumerate(CG)]
    gmix = ar.alloc("gmix", [128, D], F32)
    cmask = ar.alloc("cmask", [128, 256], BF16)
    csc = ar.alloc("csc", [128, 48], F32)
    cg_t = ar.alloc("cg", [64, 16], F32)
    cbias = ar.alloc("cbias", [128, 3 * 8 * 128], F32)
    vqk = ar.alloc("vqk", [128, 128], F32)
    esink = ar.alloc("esink", [128, 8], F32)
    xt = [ar.alloc(f"xt{i}", [128, D], F32) for i in range(2)]
    junk32 = ar.alloc("junk32", [128, 512], F32)
    sm = ar.alloc("sm", [128, 4], F32)
    hn = ar.alloc("hn", [128, D], BF16)
    hnT = [ar.alloc(f"hnT{i}", [128, 8, 128], BF16) for i in range(2)]
    qr_ = [ar.alloc(f"qr{i}", [128, 512], BF16) for i in range(2)]
    kr_ = [ar.alloc(f"kr{i}", [128, 512], BF16) for i in range(2)]
    vr_ = [ar.alloc(f"vr{i}", [128, 512], BF16) for i in range(2)]
    eg = ar.alloc("eg", [128, 512], F32)
    gq_ = [ar.alloc(f"gq{i}", [128, 512], F32) for i in range(2)]
    ssq = ar.alloc("ssq", [128, 10], F32)
    rq = ar.alloc("rq", [128, 10], F32)
    qs_t = junk32
    qs = ar.alloc("qs", [128, 512], BF16)
    ks_t = ar.alloc("ks_t", [128, 128], F32)
    ks32 = ar.alloc("ks32", [128, 128], F32)
    ksb = ar.alloc("ksb", [128, 128], BF16)
    vs32 = ar.alloc("vs32", [128, 128], F32)
    vaug = [ar.alloc(f"vaug{i}", [128, 2, 66], BF16) for i in range(3)]
    qT_r_ = [ar.alloc(f"qT_r{i}", [64, 8, 128], BF16) for i in range(2)]
    kT_r_ = [ar.alloc(f"kT_r{i}", [64, 8, 128], BF16) for i in range(2)]
    qT_s_ = [ar.alloc(f"qT_s{i}", [64, 8, 128], BF16) for i in range(2)]
    kT_s = [ar.alloc(f"kT_s{i}", [64, 2, 128], BF16) for i in range(3)]
    scm = ar.alloc("scm", [128, 8, 128], BF16)
    sso = ar.alloc("sso", [128, 8], F32)
    fo = ar.alloc("fo", [128, 8], F32)
    ot = ar.alloc("ot", [128, 512], F32)
    mix = ar.alloc("mix", [128, D], BF16)
    tb = [ar.alloc(f"tb{i}", [128, 512], F32) for i in range(2)]
    pT = [[ar.alloc(f"pT{j}{b}", [128, 512], BF16) for b in range(2)] for j in range(2)]
    den = ar.alloc("den", [128, 8], F32)
    rden = ar.alloc("rden", [128, 8], F32)
    mark_sa = ar.mark()

    poolA = Rot([0, 1, 2])
    poolF = Rot([3, 4])
    poolB = poolF
    poolC = Rot([5, 6, 7])

    dma(p, SP, xt[0].full, xs)
    dma(p, POOL, ident.full, c_id)
    dma(p, SP, ident32.full, c_id)
    dma(p, POOL, cmask.full, c_mask)
    dma(p, SP, csc.full, c_sc)
    dma(p, SP, cg_t.full, c_g)
    dma(p, SP, gmix.full, v_gmix)
    dma(p, SP, vqk.full, v_qk)
    dma(p, SP, esink.full, v_sink)
    w_in_v = w_in.rearrange("(k p) n -> p k n", p=128)
    for g in (4, 5, 0, 1, 2, 3):
        c0, c1 = CG[g]
        dma(p, POOL, w_in_t[g].full, w_in_v[:, :, c0:c1])
    dma(p, SP, cbias.full, c_bias)
    dma(p, SP, gffn.full, v_gffn)
    w_out_v = w_out.rearrange("(k p) n -> p k n", p=128)
    for i in range(3):
        mset(p, POOL, vaug[i].full, 1.0)

    for t_ in [ident, ident32, gffn, gmix, cmask, csc, cg_t, cbias, vqk] + w_in_t:
        t_.const = True
    kscale = [csc[:, 0:8], csc[:, 24:32]]
    oscale = [csc[:, 8:16], csc[:, 32:40]]
    osc2 = [csc[:, 16:24], csc[:, 40:48]]
    cm = [cmask[:, 0:128], cmask[:, 128:256]]
    g128 = cg_t[:, 0:8]
    g8 = cg_t[:, 8:16]
    bias_cur = cbias[:, 0:1024]
    bias_prev = cbias[:, 1024:2048]
    bias_sn = cbias[:, 2048:3072]
    qg_bc = vqk[:, 0:64]
    kg_bc = vqk[:, 64:128]

    def r3(v, a, b):
        return v.rearrange("p (a b) -> p a b", a=a, b=b)

    def front(x_ap, s, hn32=None, load=True):
        if load:
            dma(p, SP, xt[s].full, x_ap)
        act(p, hn.full, xt[s].full, AF.Square, scale=1.0 / 32.0, accum_out=sm[:, 0:1])
        act(p, sm[:, 1:2], sm[:, 0:1], AF.Ln, bias=1e-6)
        act(p, sm[:, 2:3], sm[:, 1:2], AF.Exp, scale=-0.5)
        if hn32 is None:
            stt(p, DVE, hn.full, xt[s].full, sm[:, 2:3], gmix.full, ALU.mult, ALU.mult)
        else:
            stt(p, DVE, hn32.full, xt[s].full, sm[:, 2:3], gmix.full, ALU.mult, ALU.mult)
            cp(p, ACT, hn.full, hn32.full)
        b = poolF.next()
        bv = b.full.bitcast(BF16)
        for k in range(8):
            tr(p, bv[:, k * 128:(k + 1) * 128], hn[:, k * 128:(k + 1) * 128], ident.full)
        cp(p, DVE, hnT[s].full.rearrange("p k t -> p (k t)"), bv)

    def inproj_stages(s, v, vslot, need_v32=True):
        qr, kr, vr, gq = qr_[s], kr_[s], vr_[s], gq_[s]
        qT_r, kT_r, qT_s = qT_r_[s], kT_r_[s], qT_s_[s]
        pj = {}

        def group(g):
            c0, c1 = CG[g]
            b = poolA.next()
            n = c1 - c0
            for k in range(8):
                mm(p, b[:, 0:n], hnT[s][:, k, :], w_in_t[g][:, k, :], start=(k == 0), stop=(k == 7))
            pj[g] = b
            if g == 0:
                cp(p, ACT, qr.full, b.full)
            elif g == 1:
                tt(p, DVE, r3(kr.full, 8, 64), r3(b.full, 8, 64), bc(kscale[v], [128, 8, 64], 2), ALU.mult)
            elif g == 2:
                cp(p, ACT, vr.full, b.full)
            elif g == 3:
                act(p, eg.full, b.full, AF.Exp, scale=-1.0)
                act(p, eg.full, eg.full, AF.Ln, bias=1.0)
                act(p, eg.full, eg.full, AF.Exp, scale=-1.0)
                tt(p, DVE, gq.full, eg.full, b.full, ALU.mult)
            elif g == 4:
                act(p, junk32.full, b.full, AF.Square, scale=0.125)
                red(p, DVE, ssq[:, 0:8], r3(junk32.full, 8, 64))
            elif g == 5:
                act(p, junk32[:, 0:128], b[:, 0:128], AF.Square, scale=0.125)
                red(p, DVE, ssq[:, 8:10], r3(junk32[:, 0:128], 2, 64))
                act(p, rq.full, ssq.full, AF.Ln, bias=1e-6)
                act(p, rq.full, rq.full, AF.Exp, scale=-0.5)
                b4 = pj[4]
                stt(p, DVE, r3(qs.full, 8, 64), r3(b4.full, 8, 64), 0.125, bc(rq[:, 0:8], [128, 8, 64], 2), ALU.mult, ALU.mult)
                tt(p, DVE, r3(ks_t.full, 2, 64), r3(b[:, 0:128], 2, 64), bc(rq[:, 8:10], [128, 2, 64], 2), ALU.mult)
                tt(p, DVE, r3(ks32.full, 2, 64), r3(ks_t.full, 2, 64), bc(kg_bc, [128, 2, 64], 1), ALU.mult)
                tt(p, POOL, r3(ksb.full, 2, 64), r3(ks32.full, 2, 64), bc(qg_bc, [128, 2, 64], 1), ALU.mult)
                cp(p, ACT, vaug[vslot][:, :, 0:64], r3(b[:, 128:256], 2, 64))
                if need_v32:
                    cp(p, ACT, vs32.full, b[:, 128:256])

        def tgroup(src, dst, nh, eng):
            b = poolB.next()
            bv = b.full.bitcast(BF16)
            for h in range(nh):
                tr(p, bv[0:64, h * 128:(h + 1) * 128], src[:, h * 64:(h + 1) * 64], ident.full)
            cp(p, eng, dst.full.rearrange("p h t -> p (h t)"), bv[0:64, 0:nh * 128])

        def P0():
            group(4); group(5); group(0)

        def P1():
            group(1); group(2); group(3)

        def T0():
            tgroup(qr, qT_r, 8, DVE)
            tgroup(kr, kT_r, 8, ACT)

        def T1():
            tgroup(qs, qT_s, 8, DVE)
            tgroup(ksb, kT_s[vslot], 2, ACT)

        return [P0, P1, T0, T1]

    def inproj(s, v, vslot):
        for f_ in inproj_stages(s, v, vslot):
            f_()

    def ret_scores(v, s=0):
        kT_r, qT_r = kT_r_[s], qT_r_[s]
        for half in range(2):
            b = poolC.next()
            for hh in range(4):
                h = half * 4 + hh
                mm(p, b[:, hh * 128:(hh + 1) * 128], kT_r[:, h, :], qT_r[:, h, :])
            tt(p, DVE, scm[:, half * 4:(half + 1) * 4, :], r3(b.full, 4, 128), bc(cm[v], [128, 4, 128], 1), ALU.mult)

    def ret_out(ob, v, s=0):
        gq = gq_[s]
        act(p, ot.full, ob.full, AF.Square)
        red(p, DVE, sso.full, r3(ot.full, 8, 64))
        tt(p, DVE, sso.full, sso.full, osc2[v], ALU.mult)
        act(p, fo.full, sso.full, AF.Ln, bias=1e-6)
        act(p, fo.full, fo.full, AF.Exp, scale=-0.5)
        tt(p, DVE, fo.full, fo.full, oscale[v], ALU.mult)
        tt(p, DVE, r3(ot.full, 8, 64), r3(ob.full, 8, 64), bc(fo.full, [128, 8, 64], 2), ALU.mult)
        tt(p, DVE, mix[:, 0:512], ot.full, gq.full, ALU.mult)

    def swa_out(pvb, j):
        pv3 = pvb[:, 0:264].rearrange("p (g c) -> p g c", g=4, c=66)
        tt(p, DVE, den[:, 4 * j:4 * j + 4], pv3[:, :, 64], esink[:, 4 * j:4 * j + 4], ALU.add)
        recip(p, rden[:, 4 * j:4 * j + 4], den[:, 4 * j:4 * j + 4])
        tt(p, DVE, r3(mix[:, 512 + 256 * j:768 + 256 * j], 4, 64), pv3[:, :, 0:64],
           bc(rden[:, 4 * j:4 * j + 4], [128, 4, 64], 2), ALU.mult)

    def mix_T(t):
        b = poolB.next()
        bv = b.full.bitcast(BF16)
        for k in range(8):
            tr(p, bv[:, k * 128:(k + 1) * 128], mix[:, k * 128:(k + 1) * 128], ident.full)
        cp(p, ACT, mv(t), bv.rearrange("p (k t) -> p k t", k=8))

    GB = 2
    NSL = 2
    S0f = [ar.alloc(f"S0f{i}", [64, GB * 512], F32) for i in range(NSL)]
    S0b = [ar.alloc(f"S0b{i}", [64, GB * 512], BF16) for i in range(NSL)]
    ckb = ar.alloc("ckb", [128, 16, 128], BF16)
    cvb = ar.alloc("cvb", [128, 32, 66], BF16)
    kcT = ar.alloc("kcT", [64, 32, 128], BF16)
    bm = ar.alloc("bm", [128, 16], F32)
    krm = [ar.alloc(f"krm{i}", [128, 512], BF16) for i in range(2)]
    Stmp_s = ar.alloc("Stmp_s", [64, 512], F32)
    pTc = [ar.alloc(f"pTc{j}", [128, 512], BF16) for j in range(2)]
    oTs_sb = ar.alloc("oTs_sb", [65, 8, 128], F32)
    oT_sb = oTs_sb
    dummy = p.tile(None, "dummy_d2d")

    dma(p, SP, bm.full, c_bm)
    bm.const = True
    mset(p, POOL, cvb.full, 1.0)
    dma(p, POOL, ckb.full, ck.rearrange("b k f -> k b f"))
    cv_v = cv.rearrange("b k (j d) -> k b j d", j=2)
    cvb4 = cvb.full.rearrange("k (b j) d -> k b j d", j=2)
    for j in range(2):
        dma(p, POOL, cvb4[:, :, j, 0:64], cv_v[:, :, j, :])
    dma(p, SP, View(dummy, kso[:, 0:120, :]), ck[:, 8:128, :])
    dma(p, SP, View(dummy, vso[:, 0:120, :]), cv[:, 8:128, :])

    if stop == "pre":
        return nc, p.emit(), 0
    qr, kr, vr, gq = qr_[0], kr_[0], vr_[0], gq_[0]
    qT_r, kT_r, qT_s = qT_r_[0], kT_r_[0], qT_s_[0]
    front(xs, 0, load=False)
    p.add(ACT, lambda h: h.activation(esink.full.ap, esink.full.ap, AF.Exp), reads=[esink, hnT[0]], writes=[esink], tag="act", dur=0.25)
    esink.const = True
    if stop == "front":
        return nc, p.emit(), 0
    inproj(0, 1, 0)
    if stop == "inproj":
        return nc, p.emit(), 0
    dma(p, SP, kso[:, 120:128, :], ks32.full)
    dma(p, SP, vso[:, 120:128, :], vs32.full)

    ck3 = ckb.full.rearrange("k b (j d) -> k (b j) d", d=64)
    for hf_ in range(2):
        tt(p, POOL, ck3[:, hf_ * 16:(hf_ + 1) * 16, :], ck3[:, hf_ * 16:(hf_ + 1) * 16, :], bc(qg_bc, [128, 16, 64], 1), ALU.mult)
    for r in range(4):
        b = poolB.next()
        bv = b.full.bitcast(BF16)
        for i in range(8):
            idx = r * 8 + i
            tr(p, bv[0:64, i * 128:(i + 1) * 128], ckb[:, idx // 2, (idx % 2) * 64:(idx % 2) * 64 + 64], ident.full)
        cp(p, DVE if r % 2 == 0 else ACT, kcT[:, r * 8:(r + 1) * 8, :].rearrange("p a t -> p (a t)"), bv[0:64, :])

    if stop == "kct":
        return nc, p.emit(), 0
    ret_scores(1)
    oTb = [poolC.next(), poolC.next()]
    for h in range(8):
        mm(p, oTb[h // 4][0:64, (h % 4) * 128:(h % 4 + 1) * 128], vr[:, h * 64:(h + 1) * 64], scm[:, h, :],
           start=(h % 4 == 0), stop=False, skip_group_check=True)
    s0_v = s0.rearrange("b h d e -> d (b h) e")
    rs_v = rs.rearrange("b h d e -> d (b h) e")
    for bg in range(16 // GB):
        sl = bg % NSL
        S0f4 = S0f[sl].full.rearrange("d (a e) -> d a e", e=64)
        dma(p, SP, S0f4, s0_v[:, bg * GB * 8:(bg + 1) * GB * 8, :])
        dma(p, POOL, S0b[sl].full.rearrange("d (a e) -> d a e", e=64), s0_v[:, bg * GB * 8:(bg + 1) * GB * 8, :])
        for bb in range(GB):
            b_ = bg * GB + bb
            for h in range(8):
                mm(p, oTb[h // 4][0:64, (h % 4) * 128 + b_ * 8:(h % 4) * 128 + b_ * 8 + 8],
                   S0b[sl][:, (bb * 8 + h) * 64:(bb * 8 + h + 1) * 64], qT_r[:, h, b_ * 8:(b_ + 1) * 8],
                   start=False, stop=True, skip_group_check=True)
        for bb in range(GB):
            b_ = bg * GB + bb
            km = krm[b_ % 2]
            ts(p, DVE, km.full, kr.full, bm[:, b_:b_ + 1], None, ALU.mult)
            stb = poolA.next()
            for h in range(8):
                mm(p, stb[0:64, h * 64:(h + 1) * 64], km[:, h * 64:(h + 1) * 64], vr[:, h * 64:(h + 1) * 64])
            seg = S0f[sl][:, bb * 512:(bb + 1) * 512]
            tt(p, DVE, Stmp_s.full, stb[0:64, :], seg, ALU.add)
            tt(p, DVE, r3(seg, 8, 64), r3(Stmp_s.full, 8, 64), bc(g8, [64, 8, 64], 2), ALU.mult)
        dma(p, SP, rs_v[:, bg * GB * 8:(bg + 1) * GB * 8, :], S0f4)
    for half in range(2):
        cp(p, ACT, oT_sb[0:64, half * 4:(half + 1) * 4, :].rearrange("p a t -> p (a t)"), oTb[half][0:64, :])
    ob = poolC.next()
    for h in range(8):
        tr(p, ob[:, h * 64:(h + 1) * 64], oT_sb[0:64, h, :], ident32[0:64, 0:64])
    ret_out(ob, 1)

    if stop == "sret":
        return nc, p.emit(), 0
    for j in range(2):
        b = poolC.next()
        b3 = r3(b.full, 4, 128)
        for b_ in range(16):
            mm(p, b3[:, :, b_ * 8:(b_ + 1) * 8], kcT[:, b_ * 2 + j, :], qT_s[:, 4 * j:4 * j + 4, b_ * 8:(b_ + 1) * 8])
        bp = r3(bias_prev, 8, 128)[:, 4 * j:4 * j + 4, 0:8]
        bpb = View(bp.tile, bp.ap.unsqueeze(2).to_broadcast([128, 4, 16, 8]))
        tt(p, DVE, tb[j].full.rearrange("p (g b l) -> p g b l", g=4, b=16), b.full.rearrange("p (g b l) -> p g b l", g=4, b=16), bpb, ALU.add)
        act(p, pTc[j].full, tb[j].full, AF.Exp)
    for j in range(2):
        b = poolC.next()
        mm(p, b.full, kT_s[0][:, j, :], qT_s[:, 4 * j:4 * j + 4, :].rearrange("p g t -> p (g t)"))
        tt(p, DVE, tb[j].full, b.full, bias_sn[:, 512 * j:512 * (j + 1)], ALU.add)
        act(p, pT[j][0].full, tb[j].full, AF.Exp)
    for j in range(2):
        b = poolC.next()
        mm(p, b[0:65, :], vaug[0][:, j, 0:65], pT[j][0].full, start=True, stop=False, skip_group_check=True)
        b3 = r3(b.full, 4, 128)
        pc3 = r3(pTc[j].full, 4, 128)
        for b_ in range(16):
            mm(p, b3[0:65, :, b_ * 8:(b_ + 1) * 8], cvb[:, b_ * 2 + j, 0:65], pc3[:, :, b_ * 8:(b_ + 1) * 8],
               start=False, stop=True, skip_group_check=True)
        cp(p, ACT, oTs_sb[:, 4 * j:4 * j + 4, :].rearrange("p a t -> p (a t)"), b[0:65, :])
    for j in range(2):
        pvb = poolC.next()
        for g in range(4):
            tr(p, pvb[:, g * 66:g * 66 + 65], oTs_sb[:, 4 * j + g, :], ident32[0:65, 0:65])
        swa_out(pvb, j)
    mix_T(16)

    if stop == "S":
        return nc, p.emit(), 0
    ar.reset(mark_sa)
    S32 = ar.alloc("S32", [64, 512], F32)
    Sbf = ar.alloc("Sbf", [64, 512], BF16)
    Stmp = ar.alloc("Stmp", [64, 512], F32)
    mset(p, POOL, S32.full, 0.0)
    rp_v = rp.rearrange("h d e -> d h e")
    hn32T = ar.alloc("hn32T", [128, 8, 128], F32)
    w32 = [ar.alloc(f"w32_{i}", [128, 1024], F32) for i in range(2)]
    def xload(t):
        dma(p, SP, xt[t % 2].full, xp[t * 128:(t + 1) * 128, :])

    def tok0_path():
        for half in range(2):
            b = poolC.next()
            for kk in range(4):
                k = half * 4 + kk
                tr(p, b[:, kk * 128:(kk + 1) * 128], xt[1][:, k * 128:(k + 1) * 128], ident32.full)
            cp(p, ACT if half == 0 else DVE, hn32T[:, half * 4:(half + 1) * 4, :].rearrange("p a t -> p (a t)"), b.full)
        bq, bk = poolC.next(), poolC.next()
        for k in range(8):
            dma(p, SP, w32[k % 2].full, w_in[k * 128:(k + 1) * 128, 0:1024])
            mm(p, bq.full, hn32T[:, k, :], w32[k % 2][:, 0:512], start=(k == 0), stop=(k == 7))
            mm(p, bk.full, hn32T[:, k, :], w32[k % 2][:, 512:1024], start=(k == 0), stop=(k == 7))
        q32sb, k32sb = w32[0][:, 0:512], w32[0][:, 512:1024]
        cp(p, ACT, q32sb, bq.full)
        tt(p, DVE, r3(k32sb, 8, 64), r3(bk.full, 8, 64), bc(kscale[0], [128, 8, 64], 2), ALU.mult)
        for src, dst, eng in ((q32sb, hn32T, ACT), (k32sb, w32[1], DVE)):
            for half in range(2):
                b = poolC.next()
                for hh in range(4):
                    h = half * 4 + hh
                    tr(p, b[0:64, hh * 128:(hh + 1) * 128], src[:, h * 64:(h + 1) * 64], ident32.full)
                d2 = dst[0:64].rearrange("p a t -> p (a t)") if dst is hn32T else dst[0:64, :]
                cp(p, eng, d2[:, half * 512:(half + 1) * 512], b[0:64, :])

    def front_stages(t):
        s = t % 2
        if t == 0:
            def F():
                front(None, 0, hn32=xt[1], load=False)
                tok0_path()
        else:
            def F():
                front(None, s, load=False)
        st = [F] + inproj_stages(s, 0, t % 3, need_v32=(t == NT_P - 1))
        if t == NT_P - 1:
            last = st[-1]

            def T1x():
                last()
                dma(p, SP, kp, ks32.full)
                dma(p, SP, vp, vs32.full)
            st[-1] = T1x
        return st

    def back_stages(t):
        s = t % 2
        v3, v3p = t % 3, (t - 1) % 3
        qT_r, kr, vr, qT_s = qT_r_[s], kr_[s], vr_[s], qT_s_[s]
        hold = {}

        def R1():
            if t == 0:
                qT32 = hn32T[0:64]
                kT32 = w32[1][0:64, :].rearrange("p (a t) -> p a t", a=8)
                for half in range(2):
                    b = poolC.next()
                    for hh in range(4):
                        h = half * 4 + hh
                        mm(p, b[:, hh * 128:(hh + 1) * 128], kT32[:, h, :], qT32[:, h, :])
                    tt(p, DVE, scm[:, half * 4:(half + 1) * 4, :], r3(b.full, 4, 128), bc(cm[0], [128, 4, 128], 1), ALU.mult)
            else:
                ret_scores(0, s)

        def R2():
            ob = poolC.next()
            hold["ob"] = ob
            for h in range(8):
                sl_ = slice(h * 64, (h + 1) * 64)
                mm(p, ob[:, sl_], scm[:, h, :], vr[:, sl_], start=True, stop=(t == 0))
                if t > 0:
                    mm(p, ob[:, sl_], qT_r[:, h, :], Sbf[:, sl_], start=False, stop=True)
            stb = poolC.next()
            hold["stb"] = stb
            for h in range(8):
                sl_ = slice(h * 64, (h + 1) * 64)
                mm(p, stb[0:64, sl_], kr[:, sl_], vr[:, sl_])

        def R3():
            stb = hold["stb"]
            tt(p, DVE, Stmp.full, stb[0:64, :], S32.full, ALU.add)
            tt(p, POOL, r3(S32.full, 8, 64), r3(Stmp.full, 8, 64), bc(g128, [64, 8, 64], 2), ALU.mult)
            if t < NT_P - 1:
                cp(p, POOL, Sbf.full, S32.full)
            else:
                dma(p, SP, rp_v, r3(S32.full, 8, 64))
            ret_out(hold["ob"], 0, s)

        blks = ([(v3p, bias_prev, 0)] if t > 0 else []) + [(v3, bias_cur, 1)]

        def W1():
            for j in range(2):
                for (ksl, bias_v, bi) in blks:
                    b = poolC.next()
                    mm(p, b.full, kT_s[ksl][:, j, :], qT_s[:, 4 * j:4 * j + 4, :].rearrange("p g t -> p (g t)"))
                    tt(p, DVE, tb[bi].full, b.full, bias_v[:, 512 * j:512 * (j + 1)], ALU.add)
                    act(p, pT[j][bi].full, tb[bi].full, AF.Exp)

        def W2():
            for j in range(2):
                pvb = poolC.next()
                for g in range(4):
                    for n_, (ksl, bias_v, bi) in enumerate(blks):
                        mm(p, pvb[:, g * 66:g * 66 + 65], pT[j][bi][:, g * 128:(g + 1) * 128], vaug[ksl][:, j, 0:65],
                           start=(n_ == 0), stop=(n_ == len(blks) - 1))
                swa_out(pvb, j)

        def M():
            if debug and t == 0:
                dma(p, POOL, dbg_mix, mix.full)
                dma(p, SP, dbg_ot, ot.full)
                dma(p, SP, dbg_gq, gq_[0].full)
            mix_T(t)

        return [R1, R2, R3, W1, W2, M]

    xload(0)
    for i in range(-1, nA):
        fs = front_stages(i + 1) if i + 1 < nA else []
        bs = back_stages(i) if i >= 0 else []
        order = [("f", 0), ("x", 0), ("b", 0), ("f", 1), ("b", 1), ("f", 2), ("b", 2), ("b", 3), ("f", 3), ("b", 4), ("f", 4), ("b", 5)]
        for kind, idx in order:
            if kind == "x":
                if i + 2 < nA:
                    xload(i + 2)
                continue
            lst = fs if kind == "f" else bs
            if idx < len(lst):
                lst[idx]()

    if stop == "A":
        return nc, p.emit(), 0
    ar.reset((ar.base, 0))
    p.sync_lat = SYNC_B
    p.pe_scale = 1.0
    w_out_t = [ar.alloc(f"w_out{h}", [128, 8, 512], BF16) for h in range(2)]
    w_up_t = [ar.alloc(f"w_up{g}", [128, 8, 512], BF16) for g in range(8)]
    w_dn_t = [ar.alloc(f"w_dn{g}", [128, 4, D], BF16) for g in range(8)]
    xh = [ar.alloc(f"xh{i}", [128, 2, D], F32) for i in range(2)]
    smB = ar.alloc("smB", [128, 4], F32)
    hn2 = ar.alloc("hn2", [128, D], BF16)
    r1 = [ar.alloc(f"r1{i}", [128, 256], F32) for i in range(3)]
    a_t = [ar.alloc(f"a{i}", [128, 256], BF16) for i in range(4)]

    for h in range(2):
        dma(p, POOL, w_out_t[h].full, w_out_v[:, :, h * 512:(h + 1) * 512])
    w_up_v = w_up.rearrange("(k p) n -> p k n", p=128)
    w_dn_v = w_down.rearrange("(c p) n -> p c n", p=128)
    for g in range(8):
        dma(p, POOL, w_up_t[g].full, w_up_v[:, :, g * 512:(g + 1) * 512])
        dma(p, POOL, w_dn_t[g].full, w_dn_v[:, g * 4:(g + 1) * 4, :])
    for t_ in w_out_t + w_up_t + w_dn_t:
        t_.const = True

    if stop == "Bw":
        return nc, p.emit(), 0
    supers = [[2 * i, 2 * i + 1] for i in range(8)] + [[16]]
    if stop in ("Bp", "B1", "Bf"):
        supers = supers[:1]
    yb = [[banks[0], banks[1]], [banks[2], banks[3]]]
    pbank = banks[4]
    ubk = [banks[5], banks[6], banks[7]]

    def prologue_stages(si):
        X = xh[si % 2]
        st = []
        for i, t in enumerate(supers[si]):
            def A(i=i, t=t):
                x_ap = xs if t == 16 else xp[t * 128:(t + 1) * 128, :]
                dma(p, SP, X[:, i, :], x_ap)
                for half in range(2):
                    for k in range(8):
                        mm(p, pbank.full, mv(t)[:, k, :], w_out_t[half][:, k, :], start=(k == 0), stop=(k == 7))
                    tt(p, DVE, X[:, i, half * 512:(half + 1) * 512], pbank.full, X[:, i, half * 512:(half + 1) * 512], ALU.add)
                act(p, hn2[:, 0:512], X[:, i, 0:512], AF.Square, scale=1.0 / 32.0, accum_out=smB[:, 0:1])
                act(p, hn2[:, 512:1024], X[:, i, 512:1024], AF.Square, scale=1.0 / 32.0, accum_out=smB[:, 3:4])
                tt(p, DVE, smB[:, 0:1], smB[:, 0:1], smB[:, 3:4], ALU.add)
                act(p, smB[:, 1:2], smB[:, 0:1], AF.Ln, bias=1e-6)
                act(p, smB[:, 2:3], smB[:, 1:2], AF.Exp, scale=-0.5)
                for hh_ in range(2):
                    cs = slice(hh_ * 512, (hh_ + 1) * 512)
                    stt(p, DVE, hn2[:, cs], X[:, i, cs], smB[:, 2:3], gffn[:, cs], ALU.mult, ALU.mult)

            def C(i=i, t=t):
                bv = pbank.full.bitcast(BF16)
                for k in range(8):
                    tr(p, bv[:, k * 128:(k + 1) * 128], hn2[:, k * 128:(k + 1) * 128], ident.full)
                bv3 = bv.rearrange("p (k t) -> p k t", k=8)
                cp(p, ACT, mv(t)[:, 0:4, :], bv3[:, 0:4, :])
                cp(p, ACT, mv(t)[:, 4:8, :], bv3[:, 4:8, :])
            st += [A, C]
        return st

    def prologue(si):
        for f_ in prologue_stages(si):
            f_()

    def ffn(si, hooks=None):
        hooks = hooks or {}
        tl = supers[si]
        TS = len(tl)
        NTOK = TS * 128
        Hp = mixTp[si]

        def up(c):
            ub = ubk[c % 3][:, 0:NTOK]
            for k in range(8):
                mm(p, ub, w_up_t[c // 4][:, k, (c % 4) * 128:(c % 4 + 1) * 128], Hp[:, 0:TS, k, :], start=(k == 0), stop=(k == 7))

        up(0)
        up(1)
        for c in range(32):
            if c in hooks:
                hooks[c]()
            if c + 2 < 32:
                up(c + 2)
            ub = ubk[c % 3][:, 0:NTOK]
            r_ = r1[c % 3][:, 0:NTOK]
            a_ = a_t[c % 4][:, 0:NTOK]
            act(p, r_, ub, AF.Relu)
            tt(p, DVE, a_, r_, r_, ALU.mult)
            for i in range(TS):
                for half in range(2):
                    mm(p, yb[i][half].full, a_t[c % 4][:, i * 128:(i + 1) * 128], w_dn_t[c // 4][:, c % 4, half * 512:(half + 1) * 512],
                       start=(c == 0), stop=(c == 31))

    def epilogue(si):
        X = xh[si % 2]
        for i, t in enumerate(supers[si]):
            for half in range(2):
                tt(p, DVE, X[:, i, half * 512:(half + 1) * 512], yb[i][half].full, X[:, i, half * 512:(half + 1) * 512], ALU.add)
            y_ap = ys if t == 16 else yp[t * 128:(t + 1) * 128, :]
            dma(p, SP, y_ap, X[:, i, :])

    prologue(0)
    if stop == "Bp":
        return nc, p.emit(), 0
    p.pin = PIN_B
    for si in range(len(supers)):
        hooks = {}
        if si + 1 < len(supers):
            st = prologue_stages(si + 1)
            for c_, f_ in zip((1, 8, 15, 22), st):
                hooks[c_] = f_
        ffn(si, hooks)
        epilogue(si)
    p.pin = False

    stats = p.emit()
    return nc, stats, (pers.cur, ar.peak, top)


def _consts():
    f = np.float32
    H = 8
    gam = 1.0 - 2.0 ** (-5.0 - np.arange(H, dtype=np.float64))
    slopes = 2.0 ** (-8.0 * np.arange(1, H + 1, dtype=np.float64) / H)
    idx = np.arange(128)
    c_id = np.eye(128, dtype=f)
    cmask_p = (idx[None, :] >= idx[:, None]).astype(f)
    seq = idx // 8
    pos = idx % 8
    cmask_s = ((seq[None, :] == seq[:, None]) & (pos[None, :] >= pos[:, None])).astype(f)
    c_mask = np.concatenate([cmask_p, cmask_s], axis=1)

    def sc(posv):
        ks = gam[None, :] ** (-(posv[:, None] + 1.0)) * 64.0 ** -0.5
        os_ = gam[None, :] ** (posv[:, None] + 1.0)
        return [ks, os_, os_ * os_ / 64.0]

    c_sc = np.concatenate(sc(idx.astype(np.float64)) + sc(pos.astype(np.float64)), axis=1).astype(f)
    c_g = np.concatenate([np.broadcast_to(gam[None, :] ** 128.0, (64, 8)), np.broadcast_to(gam[None, :] ** 8.0, (64, 8))], axis=1).astype(f)
    NEG = -30000.0
    key = idx[:, None, None]
    q = idx[None, None, :]
    sl = slopes[None, :, None]
    b_cur = np.where(key <= q, -sl * (q - key), NEG)
    b_prev = np.where(key > q, -sl * (128 + q - key), NEG)
    kseq, kpos = seq[:, None, None], pos[:, None, None]
    qseq, qpos = seq[None, None, :], pos[None, None, :]
    b_sn = np.where((kseq == qseq) & (kpos <= qpos), -sl * (qpos - kpos), NEG)
    c_bias = np.concatenate([b_cur.reshape(128, -1), b_prev.reshape(128, -1), b_sn.reshape(128, -1)], axis=1).astype(f)
    c_bm = (seq[:, None] == np.arange(16)[None, :]).astype(f)
    return dict(c_id=c_id, c_mask=c_mask, c_sc=c_sc, c_g=c_g, c_bias=c_bias, c_bm=c_bm)


_CACHE = {}


def kernel(x_prompt, x_sample, state_ret, cache_swa_k, cache_swa_v, norm_mix_gain, w_in,
           q_norm_gain, k_norm_gain, attn_sinks, w_out, norm_ffn_gain, w_up, w_down):
    f = np.float32
    A = lambda a: np.ascontiguousarray(np.asarray(a), dtype=f)
    if "nc" not in _CACHE:
        _CACHE["nc"] = build_program()[0]
    nc = _CACHE["nc"]
    cst = _consts()
    rep = lambda v_: np.ascontiguousarray(np.broadcast_to(A(v_)[None, :], (128, A(v_).shape[0])))
    shared = dict(w_in=A(w_in), w_out=A(w_out), w_up=A(w_up), w_down=A(w_down),
                  v_gmix=rep(norm_mix_gain), v_gffn=rep(norm_ffn_gain),
                  v_qk=np.ascontiguousarray(np.concatenate([rep(q_norm_gain), rep(k_norm_gain)], axis=1)),
                  v_sink=rep(attn_sinks), **cst)
    xp_, xs_, s0_, ck_, cv_ = A(x_prompt), A(x_sample), A(state_ret), A(cache_swa_k), A(cache_swa_v)
    in_maps = []
    for c in range(NCORES):
        m = dict(shared)
        m["xp"] = xp_[c]
        m["xs"] = xs_[16 * c:16 * c + 16].reshape(128, D)
        m["s0"] = s0_[16 * c:16 * c + 16]
        m["ck"] = ck_[16 * c:16 * c + 16].reshape(16, 128, 128)
        m["cv"] = cv_[16 * c:16 * c + 16].reshape(16, 128, 128)
        in_maps.append(m)
    res = run_bass_kernel_spmd(nc, in_maps, core_ids=list(range(NCORES)))
    R = res.results
    y_prompt = np.stack([R[c]["yp"] for c in range(NCORES)]).reshape(8, 2048, D)
    y_sample = np.concatenate([R[c]["ys"].reshape(16, 8, D) for c in range(NCORES)])
    ret_p = np.stack([R[c]["rp"] for c in range(NCORES)]).reshape(8, 8, 64, 64)
    k_p = np.stack([R[c]["kp"] for c in range(NCORES)]).reshape(8, 128, 2, 64)
    v_p = np.stack([R[c]["vp"] for c in range(NCORES)]).reshape(8, 128, 2, 64)
    ret_s = np.concatenate([R[c]["rs"] for c in range(NCORES)]).reshape(128, 8, 64, 64)
    k_s = np.concatenate([R[c]["kso"] for c in range(NCORES)]).reshape(128, 128, 2, 64)
    v_s = np.concatenate([R[c]["vso"] for c in range(NCORES)]).reshape(128, 128, 2, 64)
    return tuple(np.asarray(a, dtype=f) for a in (y_prompt, y_sample, ret_p, k_p, v_p, ret_s, k_s, v_s))
```

```python
import numpy as np
import concourse.bass as bass
import concourse.mybir as mybir

F32 = mybir.dt.float32
BF16 = mybir.dt.bfloat16
AF = mybir.ActivationFunctionType
ALU = mybir.AluOpType
AX = mybir.AxisListType

SAME_ENGINE_WAR_SEM = False
PE, ACT, DVE, POOL, SP = "pe", "act", "dve", "pool", "sp"
COMPUTE = (PE, ACT, DVE, POOL)
ENGS = (PE, ACT, DVE, POOL, SP)


class Tile:
    def __init__(self, prog, handle, name):
        self.prog = prog
        self.h = handle
        self.name = name
        self.last_w = []
        self.readers = []
        self.last_acc = {}
        self.const = False
        self.excl = False
        self.sem = None
        self.ndma = 0

    def __getitem__(self, key):
        return View(self, self.h.ap()[key])

    @property
    def full(self):
        return View(self, self.h.ap())


class View:
    def __init__(self, tile, ap):
        self.tile = tile
        self.ap = ap

    def __getitem__(self, key):
        return View(self.tile, self.ap[key])

    def bitcast(self, dt):
        return View(self.tile, self.ap.bitcast(dt))

    def rearrange(self, s, **kw):
        return View(self.tile, self.ap.rearrange(s, **kw))


class Op:
    __slots__ = ("eng", "lidx", "fn", "preds", "raw", "signal", "is_dma", "sem", "target", "tag",
                 "dur", "lat", "pos", "fin", "nsucc", "sync", "cyc")

    def __init__(self, eng, lidx, fn, tag=""):
        self.eng = eng
        self.lidx = lidx
        self.fn = fn
        self.preds = set()
        self.raw = set()
        self.signal = False
        self.is_dma = False
        self.sem = None
        self.target = 0
        self.tag = tag
        self.dur = 0.1
        self.lat = 0.0
        self.pos = -1
        self.fin = 0.0
        self.cyc = 0.0


class Prog:
    def __init__(self, nc, schedule=True, window=200):
        self.nc = nc
        self.all = []
        self.tiles = []
        self.schedule = schedule
        self.window = window
        self.pin = False
        self.last_pin = {}
        self.sync_lat = 1.2
        self.pe_scale = 1.0
        self.pace = False
        self.pace_hi = 0.55
        self.pace_lo = 0.42

    def tile(self, handle, name):
        t = Tile(self, handle, name)
        self.tiles.append(t)
        return t

    def add(self, eng, fn, reads=(), writes=(), dma=False, tag="", dur=0.1, lat=0.0, cyc=0.0):
        op = Op(eng, len(self.all), fn, tag)
        op.cyc = cyc
        op.is_dma = dma
        op.dur = dur * (self.pe_scale if eng == PE else 1.0)
        op.lat = lat
        op.sync = self.sync_lat
        rt, wt = [], []
        for x in reads:
            t = x.tile if isinstance(x, View) else x
            if t is not None and t not in rt:
                rt.append(t)
        for x in writes:
            t = x.tile if isinstance(x, View) else x
            if t is not None and t not in wt:
                wt.append(t)
        for t in rt:
            if t in wt:
                continue
            for w in t.last_w:
                op.preds.add(w)
                op.raw.add(w)
            if t.excl:
                for r in t.readers:
                    if r.eng != eng:
                        op.preds.add(r)
        for t in wt:
            for w in t.last_w:
                op.preds.add(w)
                if t in rt or SAME_ENGINE_WAR_SEM:
                    op.raw.add(w)
            for r in t.readers:
                op.preds.add(r)
                if SAME_ENGINE_WAR_SEM:
                    op.raw.add(r)
        for t in rt + wt:
            if not t.const:
                prev = t.last_acc.get(eng)
                if prev is not None:
                    op.preds.add(prev)
                t.last_acc[eng] = op
        if self.pin:
            prev = self.last_pin.get(eng)
            if prev is not None:
                op.preds.add(prev)
            self.last_pin[eng] = op
        op.preds.discard(op)
        if dma:
            owner = None
            for t in wt + rt:
                if t.h is not None:
                    owner = t
                    break
            if owner is None:
                owner = (wt + rt)[0]
            op.sem = owner
            owner.ndma += 1
            op.target = 16 * owner.ndma
        for t in rt:
            if t not in wt:
                t.readers.append(op)
        for t in wt:
            t.last_w = [op]
            t.readers = []
        self.all.append(op)
        return op

    @staticmethod
    def _needs_sem(op, q):
        if q.is_dma:
            return True
        if q.eng != op.eng or op.is_dma:
            return True
        return (q in op.raw) and op.eng != PE

    def _schedule(self):
        order = {e: [] for e in ENGS}
        if not self.schedule:
            for op in self.all:
                op.pos = len(order[op.eng])
                order[op.eng].append(op)
            return order
        succs = {}
        npend = {}
        ready = {}
        for op in self.all:
            npend[op] = len(op.preds)
            ready[op] = 0.0
            for q in sorted(op.preds, key=lambda o: o.lidx):
                succs.setdefault(q, []).append(op)
        tail = {}
        for op in reversed(self.all):
            t_ = 0.0
            for s_ in succs.get(op, ()):
                v = tail[s_] + (s_.sync if self._needs_sem(s_, op) else 0.0)
                if v > t_:
                    t_ = v
            tail[op] = t_ + op.dur + op.lat
        pend = {e: [op for op in self.all if op.eng == e] for e in ENGS}
        head = {e: 0 for e in ENGS}
        t_free = {e: 0.0 for e in ENGS}
        nleft = len(self.all)
        dma_free = 0.0
        hist = []
        PACE_W = 3.4
        while nleft:
            best, best_t, best_i = None, None, None
            for e in ENGS:
                lst = pend[e]
                h = head[e]
                while h < len(lst) and lst[h] is None:
                    h += 1
                head[e] = h
                cnt = 0
                i = h
                tf = t_free[e]
                pe_cand = None
                while i < len(lst) and cnt < self.window:
                    op = lst[i]
                    i += 1
                    if op is None:
                        continue
                    cnt += 1
                    if npend[op] > 0:
                        continue
                    st = ready[op] if ready[op] > tf else tf
                    if e == PE and self.pace and st <= tf + 1e-9:
                        if pe_cand is None:
                            pe_cand = []
                        pe_cand.append((op, st, i - 1))
                        continue
                    if best is None or st < best_t - 1e-9 or (abs(st - best_t) <= 1e-9 and tail[op] > tail[best]):
                        best, best_t, best_i = op, st, i - 1
                if e == PE and pe_cand:
                    while hist and hist[0][0] < tf - PACE_W:
                        hist.pop(0)
                    ratio = sum(c for _, c in hist) / (PACE_W * 2400.0)
                    if ratio > self.pace_hi:
                        pick = min(pe_cand, key=lambda x: (x[0].cyc / max(x[0].dur, 1e-3), -tail[x[0]]))
                    elif ratio < self.pace_lo:
                        pick = max(pe_cand, key=lambda x: (x[0].cyc / max(x[0].dur, 1e-3), tail[x[0]]))
                    else:
                        pick = max(pe_cand, key=lambda x: tail[x[0]])
                    op, st, ii = pick
                    if best is None or st < best_t - 1e-9 or (abs(st - best_t) <= 1e-9 and tail[op] > tail[best]):
                        best, best_t, best_i = op, st, ii
            assert best is not None, "scheduler deadlock"
            e = best.eng
            best.pos = len(order[e])
            order[e].append(best)
            t_free[e] = best_t + best.dur
            best.fin = best_t + best.dur + best.lat
            if e == PE:
                hist.append((t_free[e], best.cyc))
            pend[e][best_i] = None
            nleft -= 1
            for s_ in succs.get(best, ()):
                npend[s_] -= 1
                r = best.fin + (s_.sync if self._needs_sem(s_, best) else 0.0)
                if r > ready[s_]:
                    ready[s_] = r
        self.est = max(t_free.values())
        return order

    def emit(self):
        nc = self.nc
        order = self._schedule()
        for op in self.all:
            for q in op.preds:
                if q.is_dma:
                    continue
                if self._needs_sem(op, q):
                    q.signal = True
                else:
                    assert q.pos < op.pos, ("same-engine order violated", q.tag, op.tag)
        cnt = {}
        for e in ENGS:
            c = 0
            arr = []
            for op in order[e]:
                if op.signal and not op.is_dma:
                    c += 1
                arr.append(c)
            cnt[e] = arr
        sems = {e: nc.alloc_semaphore(name=f"s_{e}") for e in ENGS}
        for t in self.tiles:
            if t.ndma > 0:
                t.sem = nc.alloc_semaphore(name=f"d_{t.name}")
        stats = {e: [0, 0] for e in ENGS}
        with nc.Block() as block:
            def run(e):
                def body(h):
                    seen = {}
                    seen_d = {}
                    for op in order[e]:
                        need = {}
                        for q in sorted(op.preds, key=lambda o: o.lidx):
                            if q.is_dma:
                                k = id(q.sem)
                                if seen_d.get(k, 0) < q.target:
                                    h.wait_ge(q.sem.sem, q.target)
                                    seen_d[k] = q.target
                                    stats[e][1] += 1
                            elif self._needs_sem(op, q):
                                v = cnt[q.eng][q.pos]
                                if need.get(q.eng, 0) < v:
                                    need[q.eng] = v
                        for de, v in need.items():
                            if seen.get(de, 0) < v:
                                h.wait_ge(sems[de], v)
                                seen[de] = v
                                stats[e][1] += 1
                        ins = op.fn(h)
                        stats[e][0] += 1
                        if op.is_dma:
                            ins.then_inc(op.sem.sem, 16)
                        elif op.signal:
                            ins.then_inc(sems[e], 1)
                    for op in order[e]:
                        if op.is_dma:
                            k = id(op.sem)
                            if seen_d.get(k, 0) < op.target:
                                h.wait_ge(op.sem.sem, op.target)
                                seen_d[k] = op.target
                return body

            block.tensor(run(PE))
            block.scalar(run(ACT))
            block.vector(run(DVE))
            block.gpsimd(run(POOL))
            block.sync(run(SP))
        self.stats = stats
        return stats


def _ap(x):
    return x.ap if isinstance(x, View) else x


def _views(*xs):
    return [x for x in xs if isinstance(x, View)]


def _fd(x):
    a = _ap(x)
    n = 1
    for d in a.shape[1:]:
        n *= d
    return n


def _cost(eng, fd, f32=True):
    if eng == ACT:
        return 0.22 + fd * 0.00083
    if eng == DVE:
        return 0.08 + fd * (0.00115 if f32 else 0.0007)
    if eng == POOL:
        return 0.25 + fd * 0.0035
    return 0.1


def mm(p, out, lhsT, rhs, start=True, stop=True, **kw):
    n = _fd(rhs)
    f32 = _ap(rhs).dtype == F32
    d = (max(n, 64) / 2400.0 + 0.03) * (4.0 if f32 else 1.0)
    return p.add(PE, lambda h: h.matmul(_ap(out), _ap(lhsT), _ap(rhs), start=start, stop=stop, **kw),
                 reads=_views(lhsT, rhs), writes=_views(out), tag="mm", dur=d, lat=0.25,
                 cyc=n * (4.0 if f32 else 1.0))


def tr(p, out, in_, ident):
    f32 = _ap(in_).dtype == F32
    d = 0.12 * (4.0 if f32 else 1.0)
    return p.add(PE, lambda h: h.transpose(_ap(out), _ap(in_), _ap(ident)),
                 reads=_views(in_, ident), writes=_views(out), tag="tr", dur=d, lat=0.25, cyc=128.0)


def act(p, out, in_, func, bias=None, scale=None, accum_out=None, eng=ACT):
    kw = {}
    if bias is not None:
        kw["bias"] = _ap(bias)
    if scale is not None:
        kw["scale"] = _ap(scale)
    if accum_out is not None:
        kw["accum_out"] = _ap(accum_out)
    return p.add(eng, lambda h: h.activation(_ap(out), _ap(in_), func, **kw),
                 reads=_views(in_, bias, scale), writes=_views(out, accum_out), tag="act", dur=_cost(ACT, _fd(out)))


def tt(p, eng, out, in0, in1, op):
    return p.add(eng, lambda h: h.tensor_tensor(_ap(out), _ap(in0), _ap(in1), op),
                 reads=_views(in0, in1), writes=_views(out), tag="tt", dur=_cost(eng, _fd(out)))


def ts(p, eng, out, in0, s1, s2, op0, op1=None, accum_out=None):
    kw = {}
    if op1 is not None:
        kw["op1"] = op1
    if accum_out is not None:
        kw["accum_out"] = _ap(accum_out)
    return p.add(eng, lambda h: h.tensor_scalar(_ap(out), _ap(in0), _ap(s1), _ap(s2) if s2 is not None else None, op0, **kw),
                 reads=_views(in0, s1, s2), writes=_views(out, accum_out), tag="ts", dur=_cost(eng, _fd(out), False))


def stt(p, eng, out, in0, scalar, in1, op0, op1):
    return p.add(eng, lambda h: h.scalar_tensor_tensor(_ap(out), _ap(in0), _ap(scalar), _ap(in1), op0, op1),
                 reads=_views(in0, scalar, in1), writes=_views(out), tag="stt", dur=_cost(eng, _fd(out)))


def cp(p, eng, out, in_):
    d = _cost(eng, _fd(out), _ap(in_).dtype == F32)
    if eng == ACT:
        return p.add(eng, lambda h: h.copy(_ap(out), _ap(in_)), reads=_views(in_), writes=_views(out), tag="cp", dur=d)
    return p.add(eng, lambda h: h.tensor_copy(_ap(out), _ap(in_)), reads=_views(in_), writes=_views(out), tag="cp", dur=d)


def red(p, eng, out, in_, op=None, axis=None):
    op = op or ALU.add
    axis = axis or AX.X
    return p.add(eng, lambda h: h.tensor_reduce(_ap(out), _ap(in_), axis, op),
                 reads=_views(in_), writes=_views(out), tag="red", dur=_cost(eng, _fd(in_)))


def mset(p, eng, out, val):
    return p.add(eng, lambda h: h.memset(_ap(out), val), reads=(), writes=_views(out), tag="mset", dur=_cost(eng, _fd(out), False) * 0.5)


def recip(p, out, in_):
    return p.add(DVE, lambda h: h.reciprocal(_ap(out), _ap(in_)), reads=_views(in_), writes=_views(out), tag="rcp",
                 dur=0.1 + _fd(out) * 0.0065)


def dma(p, eng, out, in_, **kw):
    a = _ap(out)
    nbytes = 1
    for d in a.shape:
        nbytes *= d
    nbytes *= 4
    issue = 1.1 if eng == POOL else 0.12
    return p.add(eng, lambda h: h.dma_start(out=_ap(out), in_=_ap(in_), **kw),
                 reads=_views(in_), writes=_views(out), dma=True, tag="dma", dur=issue, lat=2.0 + nbytes / 150e3)


from concourse.bass_utils import run_bass_kernel_spmd

NCORES = 8
PACE = False
SYNC_SA = 1.2
SYNC_B = 1.2
PIN_B = False
D = 1024
NT_P = 16
NTILES = 17
INW = 2816
DFF = 4096
CG = [(0, 512), (512, 1024), (1024, 1536), (1536, 2048), (2048, 2560), (2560, 2816)]


def _dsize(dt):
    return 4 if dt == F32 else 2


class Arena:
    def __init__(self, nc, prog, base, top):
        self.nc, self.p, self.base, self.top, self.cur = nc, prog, base, top, base
        self.live = []
        self.dead = []
        self.peak = base

    def alloc(self, name, shape, dt):
        nb = int(np.prod(shape[1:])) * _dsize(dt)
        off = (self.cur + 31) // 32 * 32
        assert off + nb <= self.top, f"SBUF arena overflow at {name}: {off + nb - self.top} B over"
        h = self.nc.alloc_sbuf_tensor_at(name, list(shape), dt, offset=off)
        self.cur = off + nb
        self.peak = max(self.peak, self.cur)
        t = self.p.tile(h, name)
        t.off, t.nb = off, nb
        seen_ = set()
        for (o_, n_, ops_) in self.dead:
            if o_ < off + nb and off < o_ + n_:
                for q in ops_:
                    if id(q) not in seen_:
                        seen_.add(id(q))
                        t.readers.append(q)
        self.live.append(t)
        return t

    def mark(self):
        return (self.cur, len(self.live))

    def reset(self, mark):
        cur, n = mark
        for t in self.live[n:]:
            self.dead.append((t.off, t.nb, list(t.last_w) + list(t.readers)))
        del self.live[n:]
        self.cur = cur


def bc(view, shape, axis):
    return View(view.tile, view.ap.unsqueeze(axis).to_broadcast(list(shape)))


def build_program(debug=False, stop=None, nA=NT_P, skipS=False, schedule=True, window=200):
    nc = bass.Bass("TRN2", target_bir_lowering=False)
    p = Prog(nc, schedule=schedule, window=window)
    p.sync_lat = SYNC_SA
    p.pe_scale = 1.3
    p.pace = PACE

    def din(name, shape, dt=F32):
        return nc.dram_tensor(name, list(shape), dt, kind="ExternalInput").ap()

    def dout(name, shape):
        return nc.dram_tensor(name, list(shape), F32, kind="ExternalOutput").ap()

    xp = din("xp", [2048, D]); xs = din("xs", [128, D])
    s0 = din("s0", [16, 8, 64, 64]); ck = din("ck", [16, 128, 128]); cv = din("cv", [16, 128, 128])
    w_in = din("w_in", [D, INW]); w_out = din("w_out", [D, D]); w_up = din("w_up", [D, DFF]); w_down = din("w_down", [DFF, D])
    v_gmix = din("v_gmix", [128, D]); v_gffn = din("v_gffn", [128, D])
    v_qk = din("v_qk", [128, 128]); v_sink = din("v_sink", [128, 8])
    c_id = din("c_id", [128, 128]); c_mask = din("c_mask", [128, 256]); c_sc = din("c_sc", [128, 48])
    c_g = din("c_g", [64, 16]); c_bias = din("c_bias", [128, 3 * 8 * 128]); c_bm = din("c_bm", [128, 16])
    yp = dout("yp", [2048, D]); ys = dout("ys", [128, D])
    rp = dout("rp", [8, 64, 64]); kp = dout("kp", [128, 128]); vp = dout("vp", [128, 128])
    rs = dout("rs", [16, 8, 64, 64]); kso = dout("kso", [16, 128, 128]); vso = dout("vso", [16, 128, 128])
    if debug:
        dbg_mix = dout("dbg_mix", [128, D]); dbg_ot = dout("dbg_ot", [128, 512]); dbg_gq = dout("dbg_gq", [128, 512])

    base = (nc.sbuf_base + 31) // 32 * 32
    top = nc.sbuf_top
    pers = Arena(nc, p, base, top)
    ident = pers.alloc("ident", [128, 128], BF16)
    ident32 = pers.alloc("ident32", [128, 128], F32)
    gffn = pers.alloc("gffn", [128, D], F32)
    mixTp = [pers.alloc(f"mixTp{i}", [128, 2 if i < 8 else 1, 8, 128], BF16) for i in range(9)]

    def mv(t):
        return mixTp[t // 2][:, t % 2]
    ar = Arena(nc, p, pers.cur, top)

    banks = [p.tile(nc.alloc_psum_tensor(f"bank{i}", [128, 512], F32), f"bank{i}") for i in range(8)]
    for b_ in banks:
        b_.excl = True

    class Rot:
        def __init__(self, idx):
            self.idx, self.i = idx, 0

        def next(self):
            b = banks[self.idx[self.i % len(self.idx)]]
            self.i += 1
            return b

    w_in_t = [ar.alloc(f"w_in{g}", [128, 8, c1 - c0], BF16) for g, (c0, c1) in enumerate(CG)]
    gmix = ar.alloc("gmix", [128, D], F32)
    cmask = ar.alloc("cmask", [128, 256], BF16)
    csc = ar.alloc("csc", [128, 48], F32)
    cg_t = ar.alloc("cg", [64, 16], F32)
    cbias = ar.alloc("cbias", [128, 3 * 8 * 128], F32)
    vqk = ar.alloc("vqk", [128, 128], F32)
    esink = ar.alloc("esink", [128, 8], F32)
    xt = [ar.alloc(f"xt{i}", [128, D], F32) for i in range(2)]
    junk32 = ar.alloc("junk32", [128, 512], F32)
    sm = ar.alloc("sm", [128, 4], F32)
    hn = ar.alloc("hn", [128, D], BF16)
    hnT = [ar.alloc(f"hnT{i}", [128, 8, 128], BF16) for i in range(2)]
    qr_ = [ar.alloc(f"qr{i}", [128, 512], BF16) for i in range(2)]
    kr_ = [ar.alloc(f"kr{i}", [128, 512], BF16) for i in range(2)]
    vr_ = [ar.alloc(f"vr{i}", [128, 512], BF16) for i in range(2)]
    eg = ar.alloc("eg", [128, 512], F32)
    gq_ = [ar.alloc(f"gq{i}", [128, 512], F32) for i in range(2)]
    ssq = ar.alloc("ssq", [128, 10], F32)
    rq = ar.alloc("rq", [128, 10], F32)
    qs_t = junk32
    qs = ar.alloc("qs", [128, 512], BF16)
    ks_t = ar.alloc("ks_t", [128, 128], F32)
    ks32 = ar.alloc("ks32", [128, 128], F32)
    ksb = ar.alloc("ksb", [128, 128], BF16)
    vs32 = ar.alloc("vs32", [128, 128], F32)
    vaug = [ar.alloc(f"vaug{i}", [128, 2, 66], BF16) for i in range(3)]
    qT_r_ = [ar.alloc(f"qT_r{i}", [64, 8, 128], BF16) for i in range(2)]
    kT_r_ = [ar.alloc(f"kT_r{i}", [64, 8, 128], BF16) for i in range(2)]
    qT_s_ = [ar.alloc(f"qT_s{i}", [64, 8, 128], BF16) for i in range(2)]
    kT_s = [ar.alloc(f"kT_s{i}", [64, 2, 128], BF16) for i in range(3)]
    scm = ar.alloc("scm", [128, 8, 128], BF16)
    sso = ar.alloc("sso", [128, 8], F32)
    fo = ar.alloc("fo", [128, 8], F32)
    ot = ar.alloc("ot", [128, 512], F32)
    mix = ar.alloc("mix", [128, D], BF16)
    tb = [ar.alloc(f"tb{i}", [128, 512], F32) for i in range(2)]
    pT = [[ar.alloc(f"pT{j}{b}", [128, 512], BF16) for b in range(2)] for j in range(2)]
    den = ar.alloc("den", [128, 8], F32)
    rden = ar.alloc("rden", [128, 8], F32)
    mark_sa = ar.mark()

    poolA = Rot([0, 1, 2])
    poolF = Rot([3, 4])
    poolB = poolF
    poolC = Rot([5, 6, 7])

    dma(p, SP, xt[0].full, xs)
    dma(p, POOL, ident.full, c_id)
    dma(p, SP, ident32.full, c_id)
    dma(p, POOL, cmask.full, c_mask)
    dma(p, SP, csc.full, c_sc)
    dma(p, SP, cg_t.full, c_g)
    dma(p, SP, gmix.full, v_gmix)
    dma(p, SP, vqk.full, v_qk)
    dma(p, SP, esink.full, v_sink)
    w_in_v = w_in.rearrange("(k p) n -> p k n", p=128)
    for g in (4, 5, 0, 1, 2, 3):
        c0, c1 = CG[g]
        dma(p, POOL, w_in_t[g].full, w_in_v[:, :, c0:c1])
    dma(p, SP, cbias.full, c_bias)
    dma(p, SP, gffn.full, v_gffn)
    w_out_v = w_out.rearrange("(k p) n -> p k n", p=128)
    for i in range(3):
        mset(p, POOL, vaug[i].full, 1.0)

    for t_ in [ident, ident32, gffn, gmix, cmask, csc, cg_t, cbias, vqk] + w_in_t:
        t_.const = True
    kscale = [csc[:, 0:8], csc[:, 24:32]]
    oscale = [csc[:, 8:16], csc[:, 32:40]]
    osc2 = [csc[:, 16:24], csc[:, 40:48]]
    cm = [cmask[:, 0:128], cmask[:, 128:256]]
    g128 = cg_t[:, 0:8]
    g8 = cg_t[:, 8:16]
    bias_cur = cbias[:, 0:1024]
    bias_prev = cbias[:, 1024:2048]
    bias_sn = cbias[:, 2048:3072]
    qg_bc = vqk[:, 0:64]
    kg_bc = vqk[:, 64:128]

    def r3(v, a, b):
        return v.rearrange("p (a b) -> p a b", a=a, b=b)

    def front(x_ap, s, hn32=None, load=True):
        if load:
            dma(p, SP, xt[s].full, x_ap)
        act(p, hn.full, xt[s].full, AF.Square, scale=1.0 / 32.0, accum_out=sm[:, 0:1])
        act(p, sm[:, 1:2], sm[:, 0:1], AF.Ln, bias=1e-6)
        act(p, sm[:, 2:3], sm[:, 1:2], AF.Exp, scale=-0.5)
        if hn32 is None:
            stt(p, DVE, hn.full, xt[s].full, sm[:, 2:3], gmix.full, ALU.mult, ALU.mult)
        else:
            stt(p, DVE, hn32.full, xt[s].full, sm[:, 2:3], gmix.full, ALU.mult, ALU.mult)
            cp(p, ACT, hn.full, hn32.full)
        b = poolF.next()
        bv = b.full.bitcast(BF16)
        for k in range(8):
            tr(p, bv[:, k * 128:(k + 1) * 128], hn[:, k * 128:(k + 1) * 128], ident.full)
        cp(p, DVE, hnT[s].full.rearrange("p k t -> p (k t)"), bv)

    def inproj_stages(s, v, vslot, need_v32=True):
        qr, kr, vr, gq = qr_[s], kr_[s], vr_[s], gq_[s]
        qT_r, kT_r, qT_s = qT_r_[s], kT_r_[s], qT_s_[s]
        pj = {}

        def group(g):
            c0, c1 = CG[g]
            b = poolA.next()
            n = c1 - c0
            for k in range(8):
                mm(p, b[:, 0:n], hnT[s][:, k, :], w_in_t[g][:, k, :], start=(k == 0), stop=(k == 7))
            pj[g] = b
            if g == 0:
                cp(p, ACT, qr.full, b.full)
            elif g == 1:
                tt(p, DVE, r3(kr.full, 8, 64), r3(b.full, 8, 64), bc(kscale[v], [128, 8, 64], 2), ALU.mult)
            elif g == 2:
                cp(p, ACT, vr.full, b.full)
            elif g == 3:
                act(p, eg.full, b.full, AF.Exp, scale=-1.0)
                act(p, eg.full, eg.full, AF.Ln, bias=1.0)
                act(p, eg.full, eg.full, AF.Exp, scale=-1.0)
                tt(p, DVE, gq.full, eg.full, b.full, ALU.mult)
            elif g == 4:
                act(p, junk32.full, b.full, AF.Square, scale=0.125)
                red(p, DVE, ssq[:, 0:8], r3(junk32.full, 8, 64))
                act(p, rq[:, 0:8], ssq[:, 0:8], AF.Ln, bias=1e-6)
                act(p, rq[:, 0:8], rq[:, 0:8], AF.Exp, scale=-0.5)
                stt(p, DVE, r3(qs.full, 8, 64), r3(b.full, 8, 64), 0.125, bc(rq[:, 0:8], [128, 8, 64], 2), ALU.mult, ALU.mult)
            elif g == 5:
                act(p, junk32[:, 0:128], b[:, 0:128], AF.Square, scale=0.125)
                red(p, DVE, ssq[:, 8:10], r3(junk32[:, 0:128], 2, 64))
                act(p, rq[:, 8:10], ssq[:, 8:10], AF.Ln, bias=1e-6)
                act(p, rq[:, 8:10], rq[:, 8:10], AF.Exp, scale=-0.5)
                tt(p, DVE, r3(ks_t.full, 2, 64), r3(b[:, 0:128], 2, 64), bc(rq[:, 8:10], [128, 2, 64], 2), ALU.mult)
                tt(p, DVE, r3(ks32.full, 2, 64), r3(ks_t.full, 2, 64), bc(kg_bc, [128, 2, 64], 1), ALU.mult)
                tt(p, POOL, r3(ksb.full, 2, 64), r3(ks32.full, 2, 64), bc(qg_bc, [128, 2, 64], 1), ALU.mult)
                cp(p, ACT, vaug[vslot][:, :, 0:64], r3(b[:, 128:256], 2, 64))
                if need_v32:
                    cp(p, ACT, vs32.full, b[:, 128:256])

        def tgroup(src, dst, nh, eng):
            b = poolB.next()
            bv = b.full.bitcast(BF16)
            for h in range(nh):
                tr(p, bv[0:64, h * 128:(h + 1) * 128], src[:, h * 64:(h + 1) * 64], ident.full)
            cp(p, eng, dst.full.rearrange("p h t -> p (h t)"), bv[0:64, 0:nh * 128])

        def P0():
            group(4); group(5); group(0)

        def P1():
            group(1); group(2); group(3)

        def T0():
            tgroup(qr, qT_r, 8, DVE)
            tgroup(kr, kT_r, 8, ACT)

        def T1():
            tgroup(qs, qT_s, 8, DVE)
            tgroup(ksb, kT_s[vslot], 2, ACT)

        return [P0, P1, T0, T1]

    def inproj(s, v, vslot):
        for f_ in inproj_stages(s, v, vslot):
            f_()

    def ret_scores(v, s=0):
        kT_r, qT_r = kT_r_[s], qT_r_[s]
        for half in range(2):
            b = poolC.next()
            for hh in range(4):
                h = half * 4 + hh
                mm(p, b[:, hh * 128:(hh + 1) * 128], kT_r[:, h, :], qT_r[:, h, :])
            tt(p, DVE, scm[:, half * 4:(half + 1) * 4, :], r3(b.full, 4, 128), bc(cm[v], [128, 4, 128], 1), ALU.mult)

    def ret_out(ob, v, s=0):
        gq = gq_[s]
        act(p, ot.full, ob.full, AF.Square)
        red(p, DVE, sso.full, r3(ot.full, 8, 64))
        tt(p, DVE, sso.full, sso.full, osc2[v], ALU.mult)
        act(p, fo.full, sso.full, AF.Ln, bias=1e-6)
        act(p, fo.full, fo.full, AF.Exp, scale=-0.5)
        tt(p, DVE, fo.full, fo.full, oscale[v], ALU.mult)
        tt(p, DVE, r3(ot.full, 8, 64), r3(ob.full, 8, 64), bc(fo.full, [128, 8, 64], 2), ALU.mult)
        tt(p, DVE, mix[:, 0:512], ot.full, gq.full, ALU.mult)

    def swa_out(pvb, j):
        pv3 = pvb[:, 0:264].rearrange("p (g c) -> p g c", g=4, c=66)
        tt(p, DVE, den[:, 4 * j:4 * j + 4], pv3[:, :, 64], esink[:, 4 * j:4 * j + 4], ALU.add)
        recip(p, rden[:, 4 * j:4 * j + 4], den[:, 4 * j:4 * j + 4])
        tt(p, DVE, r3(mix[:, 512 + 256 * j:768 + 256 * j], 4, 64), pv3[:, :, 0:64],
           bc(rden[:, 4 * j:4 * j + 4], [128, 4, 64], 2), ALU.mult)

    def mix_T(t):
        b = poolB.next()
        bv = b.full.bitcast(BF16)
        for k in range(8):
            tr(p, bv[:, k * 128:(k + 1) * 128], mix[:, k * 128:(k + 1) * 128], ident.full)
        cp(p, ACT, mv(t), bv.rearrange("p (k t) -> p k t", k=8))

    GB = 2
    NSL = 2
    S0f = [ar.alloc(f"S0f{i}", [64, GB * 512], F32) for i in range(NSL)]
    S0b = [ar.alloc(f"S0b{i}", [64, GB * 512], BF16) for i in range(NSL)]
    ckb = ar.alloc("ckb", [128, 16, 128], BF16)
    cvb = ar.alloc("cvb", [128, 32, 66], BF16)
    kcT = ar.alloc("kcT", [64, 32, 128], BF16)
    bm = ar.alloc("bm", [128, 16], F32)
    krm = [ar.alloc(f"krm{i}", [128, 512], BF16) for i in range(2)]
    Stmp_s = ar.alloc("Stmp_s", [64, 512], F32)
    pTc = [ar.alloc(f"pTc{j}", [128, 512], BF16) for j in range(2)]
    oTs_sb = ar.alloc("oTs_sb", [65, 8, 128], F32)
    oT_sb = oTs_sb
    dummy = p.tile(None, "dummy_d2d")

    dma(p, SP, bm.full, c_bm)
    bm.const = True
    mset(p, POOL, cvb.full, 1.0)
    dma(p, POOL, ckb.full, ck.rearrange("b k f -> k b f"))
    cv_v = cv.rearrange("b k (j d) -> k b j d", j=2)
    cvb4 = cvb.full.rearrange("k (b j) d -> k b j d", j=2)
    for j in range(2):
        dma(p, POOL, cvb4[:, :, j, 0:64], cv_v[:, :, j, :])
    dma(p, SP, View(dummy, kso[:, 0:120, :]), ck[:, 8:128, :])
    dma(p, SP, View(dummy, vso[:, 0:120, :]), cv[:, 8:128, :])

    if stop == "pre":
        return nc, p.emit(), 0
    qr, kr, vr, gq = qr_[0], kr_[0], vr_[0], gq_[0]
    qT_r, kT_r, qT_s = qT_r_[0], kT_r_[0], qT_s_[0]
    front(xs, 0, load=False)
    p.add(ACT, lambda h: h.activation(esink.full.ap, esink.full.ap, AF.Exp), reads=[esink, hnT[0]], writes=[esink], tag="act", dur=0.25)
    esink.const = True
    if stop == "front":
        return nc, p.emit(), 0
    inproj(0, 1, 0)
    if stop == "inproj":
        return nc, p.emit(), 0
    dma(p, SP, kso[:, 120:128, :], ks32.full)
    dma(p, SP, vso[:, 120:128, :], vs32.full)

    ck3 = ckb.full.rearrange("k b (j d) -> k (b j) d", d=64)
    for hf_ in range(2):
        tt(p, POOL, ck3[:, hf_ * 16:(hf_ + 1) * 16, :], ck3[:, hf_ * 16:(hf_ + 1) * 16, :], bc(qg_bc, [128, 16, 64], 1), ALU.mult)
    for r in range(4):
        b = poolB.next()
        bv = b.full.bitcast(BF16)
        for i in range(8):
            idx = r * 8 + i
            tr(p, bv[0:64, i * 128:(i + 1) * 128], ckb[:, idx // 2, (idx % 2) * 64:(idx % 2) * 64 + 64], ident.full)
        cp(p, DVE if r % 2 == 0 else ACT, kcT[:, r * 8:(r + 1) * 8, :].rearrange("p a t -> p (a t)"), bv[0:64, :])

    if stop == "kct":
        return nc, p.emit(), 0
    ret_scores(1)
    oTb = [poolC.next(), poolC.next()]
    for h in range(8):
        mm(p, oTb[h // 4][0:64, (h % 4) * 128:(h % 4 + 1) * 128], vr[:, h * 64:(h + 1) * 64], scm[:, h, :],
           start=(h % 4 == 0), stop=False, skip_group_check=True)
    s0_v = s0.rearrange("b h d e -> d (b h) e")
    rs_v = rs.rearrange("b h d e -> d (b h) e")
    for bg in range(16 // GB):
        sl = bg % NSL
        S0f4 = S0f[sl].full.rearrange("d (a e) -> d a e", e=64)
        dma(p, SP, S0f4, s0_v[:, bg * GB * 8:(bg + 1) * GB * 8, :])
        dma(p, POOL, S0b[sl].full.rearrange("d (a e) -> d a e", e=64), s0_v[:, bg * GB * 8:(bg + 1) * GB * 8, :])
        for bb in range(GB):
            b_ = bg * GB + bb
            for h in range(8):
                mm(p, oTb[h // 4][0:64, (h % 4) * 128 + b_ * 8:(h % 4) * 128 + b_ * 8 + 8],
                   S0b[sl][:, (bb * 8 + h) * 64:(bb * 8 + h + 1) * 64], qT_r[:, h, b_ * 8:(b_ + 1) * 8],
                   start=False, stop=True, skip_group_check=True)
        for bb in range(GB):
            b_ = bg * GB + bb
            km = krm[b_ % 2]
            ts(p, DVE, km.full, kr.full, bm[:, b_:b_ + 1], None, ALU.mult)
            stb = poolA.next()
            for h in range(8):
                mm(p, stb[0:64, h * 64:(h + 1) * 64], km[:, h * 64:(h + 1) * 64], vr[:, h * 64:(h + 1) * 64])
            seg = S0f[sl][:, bb * 512:(bb + 1) * 512]
            tt(p, DVE, Stmp_s.full, stb[0:64, :], seg, ALU.add)
            tt(p, DVE, r3(seg, 8, 64), r3(Stmp_s.full, 8, 64), bc(g8, [64, 8, 64], 2), ALU.mult)
        dma(p, SP, rs_v[:, bg * GB * 8:(bg + 1) * GB * 8, :], S0f4)
    for half in range(2):
        cp(p, ACT, oT_sb[0:64, half * 4:(half + 1) * 4, :].rearrange("p a t -> p (a t)"), oTb[half][0:64, :])
    ob = poolC.next()
    for h in range(8):
        tr(p, ob[:, h * 64:(h + 1) * 64], oT_sb[0:64, h, :], ident32[0:64, 0:64])
    ret_out(ob, 1)

    if stop == "sret":
        return nc, p.emit(), 0
    for j in range(2):
        b = poolC.next()
        b3 = r3(b.full, 4, 128)
        for b_ in range(16):
            mm(p, b3[:, :, b_ * 8:(b_ + 1) * 8], kcT[:, b_ * 2 + j, :], qT_s[:, 4 * j:4 * j + 4, b_ * 8:(b_ + 1) * 8])
        bp = r3(bias_prev, 8, 128)[:, 4 * j:4 * j + 4, 0:8]
        bpb = View(bp.tile, bp.ap.unsqueeze(2).to_broadcast([128, 4, 16, 8]))
        tt(p, DVE, tb[j].full.rearrange("p (g b l) -> p g b l", g=4, b=16), b.full.rearrange("p (g b l) -> p g b l", g=4, b=16), bpb, ALU.add)
        act(p, pTc[j].full, tb[j].full, AF.Exp)
    for j in range(2):
        b = poolC.next()
        mm(p, b.full, kT_s[0][:, j, :], qT_s[:, 4 * j:4 * j + 4, :].rearrange("p g t -> p (g t)"))
        tt(p, DVE, tb[j].full, b.full, bias_sn[:, 512 * j:512 * (j + 1)], ALU.add)
        act(p, pT[j][0].full, tb[j].full, AF.Exp)
    for j in range(2):
        b = poolC.next()
        mm(p, b[0:65, :], vaug[0][:, j, 0:65], pT[j][0].full, start=True, stop=False, skip_group_check=True)
        b3 = r3(b.full, 4, 128)
        pc3 = r3(pTc[j].full, 4, 128)
        for b_ in range(16):
            mm(p, b3[0:65, :, b_ * 8:(b_ + 1) * 8], cvb[:, b_ * 2 + j, 0:65], pc3[:, :, b_ * 8:(b_ + 1) * 8],
               start=False, stop=True, skip_group_check=True)
        cp(p, ACT, oTs_sb[:, 4 * j:4 * j + 4, :].rearrange("p a t -> p (a t)"), b[0:65, :])
    for j in range(2):
        pvb = poolC.next()
        for g in range(4):
            tr(p, pvb[:, g * 66:g * 66 + 65], oTs_sb[:, 4 * j + g, :], ident32[0:65, 0:65])
        swa_out(pvb, j)
    mix_T(16)

    if stop == "S":
        return nc, p.emit(), 0
    ar.reset(mark_sa)
    S32 = ar.alloc("S32", [64, 512], F32)
    Sbf = ar.alloc("Sbf", [64, 512], BF16)
    Stmp = ar.alloc("Stmp", [64, 512], F32)
    mset(p, POOL, S32.full, 0.0)
    rp_v = rp.rearrange("h d e -> d h e")
    hn32T = ar.alloc("hn32T", [128, 8, 128], F32)
    w32 = [ar.alloc(f"w32_{i}", [128, 1024], F32) for i in range(2)]
    def xload(t):
        dma(p, SP, xt[t % 2].full, xp[t * 128:(t + 1) * 128, :])

    def tok0_path():
        for half in range(2):
            b = poolC.next()
            for kk in range(4):
                k = half * 4 + kk
                tr(p, b[:, kk * 128:(kk + 1) * 128], xt[1][:, k * 128:(k + 1) * 128], ident32.full)
            cp(p, ACT if half == 0 else DVE, hn32T[:, half * 4:(half + 1) * 4, :].rearrange("p a t -> p (a t)"), b.full)
        bq, bk = poolC.next(), poolC.next()
        for k in range(8):
            dma(p, SP, w32[k % 2].full, w_in[k * 128:(k + 1) * 128, 0:1024])
            mm(p, bq.full, hn32T[:, k, :], w32[k % 2][:, 0:512], start=(k == 0), stop=(k == 7))
            mm(p, bk.full, hn32T[:, k, :], w32[k % 2][:, 512:1024], start=(k == 0), stop=(k == 7))
        q32sb, k32sb = w32[0][:, 0:512], w32[0][:, 512:1024]
        cp(p, ACT, q32sb, bq.full)
        tt(p, DVE, r3(k32sb, 8, 64), r3(bk.full, 8, 64), bc(kscale[0], [128, 8, 64], 2), ALU.mult)
        for src, dst, eng in ((q32sb, hn32T, ACT), (k32sb, w32[1], DVE)):
            for half in range(2):
                b = poolC.next()
                for hh in range(4):
                    h = half * 4 + hh
                    tr(p, b[0:64, hh * 128:(hh + 1) * 128], src[:, h * 64:(h + 1) * 64], ident32.full)
                d2 = dst[0:64].rearrange("p a t -> p (a t)") if dst is hn32T else dst[0:64, :]
                cp(p, eng, d2[:, half * 512:(half + 1) * 512], b[0:64, :])

    def front_stages(t):
        s = t % 2
        if t == 0:
            def F():
                front(None, 0, hn32=xt[1], load=False)
                tok0_path()
        else:
            def F():
                front(None, s, load=False)
        st = [F] + inproj_stages(s, 0, t % 3, need_v32=(t == NT_P - 1))
        if t == NT_P - 1:
            last = st[-1]

            def T1x():
                last()
                dma(p, SP, kp, ks32.full)
                dma(p, SP, vp, vs32.full)
            st[-1] = T1x
        return st

    def back_stages(t):
        s = t % 2
        v3, v3p = t % 3, (t - 1) % 3
        qT_r, kr, vr, qT_s = qT_r_[s], kr_[s], vr_[s], qT_s_[s]
        hold = {}

        def R1():
            if t == 0:
                qT32 = hn32T[0:64]
                kT32 = w32[1][0:64, :].rearrange("p (a t) -> p a t", a=8)
                for half in range(2):
                    b = poolC.next()
                    for hh in range(4):
                        h = half * 4 + hh
                        mm(p, b[:, hh * 128:(hh + 1) * 128], kT32[:, h, :], qT32[:, h, :])
                    tt(p, DVE, scm[:, half * 4:(half + 1) * 4, :], r3(b.full, 4, 128), bc(cm[0], [128, 4, 128], 1), ALU.mult)
            else:
                ret_scores(0, s)

        def R2():
            ob = poolC.next()
            hold["ob"] = ob
            for h in range(8):
                sl_ = slice(h * 64, (h + 1) * 64)
                mm(p, ob[:, sl_], scm[:, h, :], vr[:, sl_], start=True, stop=(t == 0))
                if t > 0:
                    mm(p, ob[:, sl_], qT_r[:, h, :], Sbf[:, sl_], start=False, stop=True)
            stb = poolC.next()
            hold["stb"] = stb
            for h in range(8):
                sl_ = slice(h * 64, (h + 1) * 64)
                mm(p, stb[0:64, sl_], kr[:, sl_], vr[:, sl_])

        def R3():
            stb = hold["stb"]
            tt(p, DVE, Stmp.full, stb[0:64, :], S32.full, ALU.add)
            tt(p, POOL, r3(S32.full, 8, 64), r3(Stmp.full, 8, 64), bc(g128, [64, 8, 64], 2), ALU.mult)
            if t < NT_P - 1:
                cp(p, POOL, Sbf.full, S32.full)
            else:
                dma(p, SP, rp_v, r3(S32.full, 8, 64))
            ret_out(hold["ob"], 0, s)

        blks = ([(v3p, bias_prev, 0)] if t > 0 else []) + [(v3, bias_cur, 1)]

        def W1():
            for j in range(2):
                for (ksl, bias_v, bi) in blks:
                    b = poolC.next()
                    mm(p, b.full, kT_s[ksl][:, j, :], qT_s[:, 4 * j:4 * j + 4, :].rearrange("p g t -> p (g t)"))
                    tt(p, DVE, tb[bi].full, b.full, bias_v[:, 512 * j:512 * (j + 1)], ALU.add)
                    act(p, pT[j][bi].full, tb[bi].full, AF.Exp)

        def W2():
            for j in range(2):
                pvb = poolC.next()
                for g in range(4):
                    for n_, (ksl, bias_v, bi) in enumerate(blks):
                        mm(p, pvb[:, g * 66:g * 66 + 65], pT[j][bi][:, g * 128:(g + 1) * 128], vaug[ksl][:, j, 0:65],
                           start=(n_ == 0), stop=(n_ == len(blks) - 1))
                swa_out(pvb, j)

        def M():
            if debug and t == 0:
                dma(p, POOL, dbg_mix, mix.full)
                dma(p, SP, dbg_ot, ot.full)
                dma(p, SP, dbg_gq, gq_[0].full)
            mix_T(t)

        return [R1, R2, R3, W1, W2, M]

    xload(0)
    for i in range(-1, nA):
        fs = front_stages(i + 1) if i + 1 < nA else []
        bs = back_stages(i) if i >= 0 else []
        order = [("f", 0), ("x", 0), ("b", 0), ("f", 1), ("b", 1), ("f", 2), ("b", 2), ("b", 3), ("f", 3), ("b", 4), ("f", 4), ("b", 5)]
        for kind, idx in order:
            if kind == "x":
                if i + 2 < nA:
                    xload(i + 2)
                continue
            lst = fs if kind == "f" else bs
            if idx < len(lst):
                lst[idx]()

    if stop == "A":
        return nc, p.emit(), 0
    ar.reset((ar.base, 0))
    p.sync_lat = SYNC_B
    p.pe_scale = 1.0
    w_out_t = [ar.alloc(f"w_out{h}", [128, 8, 512], BF16) for h in range(2)]
    w_up_t = [ar.alloc(f"w_up{g}", [128, 8, 512], BF16) for g in range(8)]
    w_dn_t = [ar.alloc(f"w_dn{g}", [128, 4, D], BF16) for g in range(8)]
    xh = [ar.alloc(f"xh{i}", [128, 2, D], F32) for i in range(2)]
    smB = ar.alloc("smB", [128, 4], F32)
    hn2 = ar.alloc("hn2", [128, D], BF16)
    r1 = [ar.alloc(f"r1{i}", [128, 256], F32) for i in range(3)]
    a_t = [ar.alloc(f"a{i}", [128, 256], BF16) for i in range(4)]

    for h in range(2):
        dma(p, POOL, w_out_t[h].full, w_out_v[:, :, h * 512:(h + 1) * 512])
    w_up_v = w_up.rearrange("(k p) n -> p k n", p=128)
    w_dn_v = w_down.rearrange("(c p) n -> p c n", p=128)
    for g in range(8):
        dma(p, POOL, w_up_t[g].full, w_up_v[:, :, g * 512:(g + 1) * 512])
        dma(p, POOL, w_dn_t[g].full, w_dn_v[:, g * 4:(g + 1) * 4, :])
    for t_ in w_out_t + w_up_t + w_dn_t:
        t_.const = True

    if stop == "Bw":
        return nc, p.emit(), 0
    supers = [[2 * i, 2 * i + 1] for i in range(8)] + [[16]]
    if stop in ("Bp", "B1", "Bf"):
        supers = supers[:1]
    yb = [[banks[0], banks[1]], [banks[2], banks[3]]]
    pbank = banks[4]
    ubk = [banks[5], banks[6], banks[7]]

    def prologue_stages(si):
        X = xh[si % 2]
        st = []
        for i, t in enumerate(supers[si]):
            def A(i=i, t=t):
                x_ap = xs if t == 16 else xp[t * 128:(t + 1) * 128, :]
                dma(p, SP, X[:, i, :], x_ap)
                for half in range(2):
                    for k in range(8):
                        mm(p, pbank.full, mv(t)[:, k, :], w_out_t[half][:, k, :], start=(k == 0), stop=(k == 7))
                    tt(p, DVE, X[:, i, half * 512:(half + 1) * 512], pbank.full, X[:, i, half * 512:(half + 1) * 512], ALU.add)
                act(p, hn2[:, 0:512], X[:, i, 0:512], AF.Square, scale=1.0 / 32.0, accum_out=smB[:, 0:1])
                act(p, hn2[:, 512:1024], X[:, i, 512:1024], AF.Square, scale=1.0 / 32.0, accum_out=smB[:, 3:4])
                tt(p, DVE, smB[:, 0:1], smB[:, 0:1], smB[:, 3:4], ALU.add)
                act(p, smB[:, 1:2], smB[:, 0:1], AF.Ln, bias=1e-6)
                act(p, smB[:, 2:3], smB[:, 1:2], AF.Exp, scale=-0.5)
                for hh_ in range(2):
                    cs = slice(hh_ * 512, (hh_ + 1) * 512)
                    stt(p, DVE, hn2[:, cs], X[:, i, cs], smB[:, 2:3], gffn[:, cs], ALU.mult, ALU.mult)

            def C(i=i, t=t):
                bv = pbank.full.bitcast(BF16)
                for k in range(8):
                    tr(p, bv[:, k * 128:(k + 1) * 128], hn2[:, k * 128:(k + 1) * 128], ident.full)
                bv3 = bv.rearrange("p (k t) -> p k t", k=8)
                cp(p, ACT, mv(t)[:, 0:4, :], bv3[:, 0:4, :])
                cp(p, ACT, mv(t)[:, 4:8, :], bv3[:, 4:8, :])
            st += [A, C]
        return st

    def prologue(si):
        for f_ in prologue_stages(si):
            f_()

    def ffn(si, hooks=None):
        hooks = hooks or {}
        tl = supers[si]
        TS = len(tl)
        NTOK = TS * 128
        Hp = mixTp[si]

        def up(c):
            ub = ubk[c % 3][:, 0:NTOK]
            for k in range(8):
                mm(p, ub, w_up_t[c // 4][:, k, (c % 4) * 128:(c % 4 + 1) * 128], Hp[:, 0:TS, k, :], start=(k == 0), stop=(k == 7))

        up(0)
        up(1)
        for c in range(32):
            if c in hooks:
                hooks[c]()
            if c + 2 < 32:
                up(c + 2)
            ub = ubk[c % 3][:, 0:NTOK]
            r_ = r1[c % 3][:, 0:NTOK]
            a_ = a_t[c % 4][:, 0:NTOK]
            act(p, r_, ub, AF.Relu)
            tt(p, DVE, a_, r_, r_, ALU.mult)
            for i in range(TS):
                for half in range(2):
                    mm(p, yb[i][half].full, a_t[c % 4][:, i * 128:(i + 1) * 128], w_dn_t[c // 4][:, c % 4, half * 512:(half + 1) * 512],
                       start=(c == 0), stop=(c == 31))

    def epilogue(si):
        X = xh[si % 2]
        for i, t in enumerate(supers[si]):
            for half in range(2):
                tt(p, DVE, X[:, i, half * 512:(half + 1) * 512], yb[i][half].full, X[:, i, half * 512:(half + 1) * 512], ALU.add)
            y_ap = ys if t == 16 else yp[t * 128:(t + 1) * 128, :]
            dma(p, SP, y_ap, X[:, i, :])

    prologue(0)
    if stop == "Bp":
        return nc, p.emit(), 0
    p.pin = PIN_B
    for si in range(len(supers)):
        hooks = {}
        if si + 1 < len(supers):
            st = prologue_stages(si + 1)
            for c_, f_ in zip((1, 8, 15, 22), st):
                hooks[c_] = f_
        ffn(si, hooks)
        epilogue(si)
    p.pin = False

    stats = p.emit()
    return nc, stats, (pers.cur, ar.peak, top)


def _consts():
    f = np.float32
    H = 8
    gam = 1.0 - 2.0 ** (-5.0 - np.arange(H, dtype=np.float64))
    slopes = 2.0 ** (-8.0 * np.arange(1, H + 1, dtype=np.float64) / H)
    idx = np.arange(128)
    c_id = np.eye(128, dtype=f)
    cmask_p = (idx[None, :] >= idx[:, None]).astype(f)
    seq = idx // 8
    pos = idx % 8
    cmask_s = ((seq[None, :] == seq[:, None]) & (pos[None, :] >= pos[:, None])).astype(f)
    c_mask = np.concatenate([cmask_p, cmask_s], axis=1)

    def sc(posv):
        ks = gam[None, :] ** (-(posv[:, None] + 1.0)) * 64.0 ** -0.5
        os_ = gam[None, :] ** (posv[:, None] + 1.0)
        return [ks, os_, os_ * os_ / 64.0]

    c_sc = np.concatenate(sc(idx.astype(np.float64)) + sc(pos.astype(np.float64)), axis=1).astype(f)
    c_g = np.concatenate([np.broadcast_to(gam[None, :] ** 128.0, (64, 8)), np.broadcast_to(gam[None, :] ** 8.0, (64, 8))], axis=1).astype(f)
    NEG = -30000.0
    key = idx[:, None, None]
    q = idx[None, None, :]
    sl = slopes[None, :, None]
    b_cur = np.where(key <= q, -sl * (q - key), NEG)
    b_prev = np.where(key > q, -sl * (128 + q - key), NEG)
    kseq, kpos = seq[:, None, None], pos[:, None, None]
    qseq, qpos = seq[None, None, :], pos[None, None, :]
    b_sn = np.where((kseq == qseq) & (kpos <= qpos), -sl * (qpos - kpos), NEG)
    c_bias = np.concatenate([b_cur.reshape(128, -1), b_prev.reshape(128, -1), b_sn.reshape(128, -1)], axis=1).astype(f)
    c_bm = (seq[:, None] == np.arange(16)[None, :]).astype(f)
    return dict(c_id=c_id, c_mask=c_mask, c_sc=c_sc, c_g=c_g, c_bias=c_bias, c_bm=c_bm)


_CACHE = {}


def kernel(x_prompt, x_sample, state_ret, cache_swa_k, cache_swa_v, norm_mix_gain, w_in,
           q_norm_gain, k_norm_gain, attn_sinks, w_out, norm_ffn_gain, w_up, w_down):
    f = np.float32
    A = lambda a: np.ascontiguousarray(np.asarray(a), dtype=f)
    if "nc" not in _CACHE:
        _CACHE["nc"] = build_program()[0]
    nc = _CACHE["nc"]
    cst = _consts()
    rep = lambda v_: np.ascontiguousarray(np.broadcast_to(A(v_)[None, :], (128, A(v_).shape[0])))
    shared = dict(w_in=A(w_in), w_out=A(w_out), w_up=A(w_up), w_down=A(w_down),
                  v_gmix=rep(norm_mix_gain), v_gffn=rep(norm_ffn_gain),
                  v_qk=np.ascontiguousarray(np.concatenate([rep(q_norm_gain), rep(k_norm_gain)], axis=1)),
                  v_sink=rep(attn_sinks), **cst)
    xp_, xs_, s0_, ck_, cv_ = A(x_prompt), A(x_sample), A(state_ret), A(cache_swa_k), A(cache_swa_v)
    in_maps = []
    for c in range(NCORES):
        m = dict(shared)
        m["xp"] = xp_[c]
        m["xs"] = xs_[16 * c:16 * c + 16].reshape(128, D)
        m["s0"] = s0_[16 * c:16 * c + 16]
        m["ck"] = ck_[16 * c:16 * c + 16].reshape(16, 128, 128)
        m["cv"] = cv_[16 * c:16 * c + 16].reshape(16, 128, 128)
        in_maps.append(m)
    res = run_bass_kernel_spmd(nc, in_maps, core_ids=list(range(NCORES)))
    R = res.results
    y_prompt = np.stack([R[c]["yp"] for c in range(NCORES)]).reshape(8, 2048, D)
    y_sample = np.concatenate([R[c]["ys"].reshape(16, 8, D) for c in range(NCORES)])
    ret_p = np.stack([R[c]["rp"] for c in range(NCORES)]).reshape(8, 8, 64, 64)
    k_p = np.stack([R[c]["kp"] for c in range(NCORES)]).reshape(8, 128, 2, 64)
    v_p = np.stack([R[c]["vp"] for c in range(NCORES)]).reshape(8, 128, 2, 64)
    ret_s = np.concatenate([R[c]["rs"] for c in range(NCORES)]).reshape(128, 8, 64, 64)
    k_s = np.concatenate([R[c]["kso"] for c in range(NCORES)]).reshape(128, 128, 2, 64)
    v_s = np.concatenate([R[c]["vso"] for c in range(NCORES)]).reshape(128, 128, 2, 64)
    return tuple(np.asarray(a, dtype=f) for a in (y_prompt, y_sample, ret_p, k_p, v_p, ret_s, k_s, v_s))
```
